# Optimizing a Trainium2 kernel written in Bass

```python
import jax, jax.numpy as jnp
from jax import lax
import numpy as np

D_MODEL = 2048
BATCH = 2
SEQ = 8192
DEPTH = 1

POOL_WINDOWS = (2, 4, 8, 16)
POOL_GROUPS = len(POOL_WINDOWS)
POOL_GROUP_WIDTH = D_MODEL // 8
POOL_WIDTH = POOL_GROUPS * POOL_GROUP_WIDTH
LRU_WIDTH = D_MODEL
LRU_BLOCK_WIDTH = 256
LRU_BLOCKS = LRU_WIDTH // LRU_BLOCK_WIDTH
LRU_CONV_WIDTH = 4
LRU_C = 8.0
LRU_A_MIN = 0.9
LRU_A_MAX = 0.999
N_BRANCHES = 2
IN_WIDTH = POOL_WIDTH + 2 * LRU_WIDTH + N_BRANCHES * D_MODEL
D_FF = 3 * D_MODEL
FFN_CONV_WIDTH = 3
EPS = 1e-6

kernel_name = "hybrid_pool_rglru_gated_block"


def rms_norm(x, g):
    xf = x.astype(jnp.float32)
    y = xf * lax.rsqrt(jnp.mean(xf * xf, axis=-1, keepdims=True) + EPS)
    return (y * g.astype(jnp.float32)).astype(x.dtype)


def causal_depthwise_conv(x, w, b):
    K = w.shape[0]
    S = x.shape[1]
    xp = jnp.pad(x, ((0, 0), (K - 1, 0), (0, 0)))
    out = b
    for k in range(K):
        out = out + xp[:, k:k + S] * w[k]
    return out


def pool_mixer(u, w_pool, pool_scale):
    B, S, _ = u.shape
    uf = u.astype(jnp.float32)
    c = jnp.cumsum(uf, axis=1)
    pos = jnp.arange(1, S + 1, dtype=jnp.float32)
    means = []
    for g, w in enumerate(POOL_WINDOWS):
        cg = c[..., g * POOL_GROUP_WIDTH:(g + 1) * POOL_GROUP_WIDTH]
        shifted = jnp.pad(cg, ((0, 0), (w, 0), (0, 0)))[:, :S]
        count = jnp.minimum(pos, float(w))[None, :, None]
        means.append((cg - shifted) / count)
    mean = jnp.stack(means, axis=2)
    d = (mean - uf.reshape(B, S, POOL_GROUPS, POOL_GROUP_WIDTH)).astype(u.dtype)
    y = jnp.einsum('bsgc,gcd->bsgd', d, w_pool).reshape(B, S, POOL_WIDTH)
    return y * pool_scale


def rg_lru(x, w_a, b_a, w_i, b_i, lam):
    B, S, R = x.shape
    xb = x.reshape(B, S, LRU_BLOCKS, LRU_BLOCK_WIDTH)
    r = jax.nn.sigmoid(jnp.einsum('bshc,hcd->bshd', xb, w_a).reshape(B, S, R) + b_a)
    i = jax.nn.sigmoid(jnp.einsum('bshc,hcd->bshd', xb, w_i).reshape(B, S, R) + b_i)
    log_a = -LRU_C * r.astype(jnp.float32) * jax.nn.softplus(-lam.astype(jnp.float32))
    a = jnp.exp(log_a)
    mult = jnp.sqrt(-jnp.expm1(2.0 * log_a))
    bx = mult * (i * x).astype(jnp.float32)

    def combine(left, right):
        a1, b1 = left
        a2, b2 = right
        return a1 * a2, a2 * b1 + b2

    _, h = lax.associative_scan(combine, (a, bx), axis=1)
    return h.astype(x.dtype)


def setup_inputs(seed: int = 0) -> dict:
    key = jax.random.key(seed)
    ks = jax.random.split(key, 24)
    f32 = jnp.float32

    def nrm(k, shape, fan_in):
        return jax.random.normal(k, shape, f32) * (fan_in ** -0.5)

    def gain(k, shape):
        return 1.0 + 0.02 * jax.random.normal(k, shape, f32)

    def bias(k, shape):
        return 0.01 * jax.random.normal(k, shape, f32)

    L = DEPTH
    u = jax.random.uniform(ks[12], (L, LRU_WIDTH), f32, LRU_A_MIN, LRU_A_MAX)
    s = u ** (1.0 / LRU_C)
    lru_lambda = jnp.log(s) - jnp.log1p(-s)
    return {
        "x": jax.random.normal(ks[0], (BATCH, SEQ, D_MODEL), f32),
        "g_mix": gain(ks[1], (L, D_MODEL)),
        "w_in": nrm(ks[2], (L, D_MODEL, IN_WIDTH), D_MODEL),
        "b_gate": bias(ks[3], (L, N_BRANCHES * D_MODEL)),
        "w_pool": nrm(ks[4], (L, POOL_GROUPS, POOL_GROUP_WIDTH, POOL_GROUP_WIDTH), POOL_GROUP_WIDTH),
        "pool_scale": gain(ks[5], (L, POOL_WIDTH)),
        "lru_conv_w": nrm(ks[6], (L, LRU_CONV_WIDTH, LRU_WIDTH), LRU_CONV_WIDTH),
        "lru_conv_b": bias(ks[7], (L, LRU_WIDTH)),
        "w_a": nrm(ks[8], (L, LRU_BLOCKS, LRU_BLOCK_WIDTH, LRU_BLOCK_WIDTH), LRU_BLOCK_WIDTH),
        "b_a": bias(ks[9], (L, LRU_WIDTH)),
        "w_i": nrm(ks[10], (L, LRU_BLOCKS, LRU_BLOCK_WIDTH, LRU_BLOCK_WIDTH), LRU_BLOCK_WIDTH),
        "b_i": bias(ks[11], (L, LRU_WIDTH)),
        "lru_lambda": lru_lambda,
        "w_pool_proj": nrm(ks[13], (L, POOL_WIDTH, D_MODEL), POOL_WIDTH),
        "w_lru_proj": nrm(ks[14], (L, LRU_WIDTH, D_MODEL), LRU_WIDTH),
        "w_out": nrm(ks[15], (L, D_MODEL, D_MODEL), D_MODEL),
        "g_mlp": gain(ks[16], (L, D_MODEL)),
        "w_up": nrm(ks[17], (L, D_MODEL, 2 * D_FF), D_MODEL),
        "ffn_conv_w": nrm(ks[18], (L, FFN_CONV_WIDTH, D_FF), FFN_CONV_WIDTH),
        "ffn_conv_b": bias(ks[19], (L, D_FF)),
        "w_down": nrm(ks[20], (L, D_FF, D_MODEL), D_FF),
        "g_final": gain(ks[21], (D_MODEL,)),
    }


def reference(x, g_mix, w_in, b_gate, w_pool, pool_scale, lru_conv_w, lru_conv_b,
              w_a, b_a, w_i, b_i, lru_lambda, w_pool_proj, w_lru_proj, w_out,
              g_mlp, w_up, ffn_conv_w, ffn_conv_b, w_down, g_final):
    B, S, D = x.shape
    for l in range(DEPTH):
        h = rms_norm(x, g_mix[l])
        proj = h @ w_in[l]
        p0 = POOL_WIDTH
        p1 = p0 + LRU_WIDTH
        p2 = p1 + LRU_WIDTH
        u_pool = proj[..., :p0]
        u_lru = proj[..., p0:p1]
        u_gelu = proj[..., p1:p2]
        gates = jax.nn.sigmoid(proj[..., p2:] + b_gate[l]).reshape(B, S, N_BRANCHES, D)

        y_pool = pool_mixer(u_pool, w_pool[l], pool_scale[l])
        v = causal_depthwise_conv(u_lru, lru_conv_w[l], lru_conv_b[l])
        y_lru = rg_lru(v, w_a[l], b_a[l], w_i[l], b_i[l], lru_lambda[l]) * jax.nn.gelu(u_gelu)

        merged = (gates[:, :, 0] * (y_pool @ w_pool_proj[l])
                  + gates[:, :, 1] * (y_lru @ w_lru_proj[l]))
        x = x + merged @ w_out[l]

        h2 = rms_norm(x, g_mlp[l])
        up = h2 @ w_up[l]
        gate_pre = up[..., :D_FF]
        val = up[..., D_FF:]
        gate = jax.nn.gelu(causal_depthwise_conv(gate_pre, ffn_conv_w[l], ffn_conv_b[l]))
        x = x + (gate * val) @ w_down[l]
    return rms_norm(x, g_final)
```

```python
import numpy as np
from contextlib import ExitStack
import concourse.bass as bass
import concourse.mybir as mybir
from concourse.bass_utils import run_bass_kernel_spmd

F32 = mybir.dt.float32
BF16 = mybir.dt.bfloat16
AF = mybir.ActivationFunctionType
ALU = mybir.AluOpType

NCORES = 8
R = 2
T = 1024
HALO = 16
D = 2048
KC = 16
NT = 8
TB = 512
DFF = 6144
EPS = 1e-6
POOL_W = (2, 4, 8, 16)

CV_GMIX, CV_GMLP, CV_BGATE, CV_PSCALE, CV_LCW, CV_LCB, CV_BA, CV_BI, CV_LAM, CV_FCW, CV_FCB, CV_SEL, CV_INVC = (
    0, 16, 32, 64, 72, 136, 152, 168, 184, 200, 344, 392, 396)
NCV = 396 + R * 4 * 16

BASE = 16512
OFF_A, OFF_B, OFF_C, OFF_RING, OFF_F = 0, 65536, 98304, 131072, 163840
NSLOT = 2
SLOT_ELEMS = 8192
UNIT = 256


class View:
    def __init__(self, nc, name, fshape, dt, off):
        self.es = 2 if dt == BF16 else 4
        self.t = nc.alloc_sbuf_tensor_at(name, [128] + list(fshape), dt, offset=BASE + off)
        self.off = off
        self.fshape = tuple(fshape)
        st = [1] * len(fshape)
        for i in range(len(fshape) - 2, -1, -1):
            st[i] = st[i + 1] * fshape[i + 1]
        self.st = st
        self._cache = {}
        assert off + self.es * int(np.prod(fshape)) <= 212832, name

    def __getitem__(self, idx):
        return self.t[idx]

    def k(self, *idx):
        if idx in self._cache:
            return self._cache[idx]
        rg = []
        for d, n in enumerate(self.fshape):
            i = idx[d] if d < len(idx) else None
            if i is None:
                rg.append((0, n))
            elif isinstance(i, int):
                rg.append((i, i + 1))
            else:
                rg.append(i)
        keys = set()

        def rec(d, base):
            if d == len(rg) - 1:
                b0 = self.off + (base + rg[d][0]) * self.es
                b1 = self.off + (base + rg[d][1]) * self.es
                keys.update(range(b0 // UNIT, (b1 + UNIT - 1) // UNIT))
            else:
                for i in range(rg[d][0], rg[d][1]):
                    rec(d + 1, base + i * self.st[d])
        rec(0, 0)
        res = tuple(keys)
        self._cache[idx] = res
        return res


class Op:
    __slots__ = ("eng", "fn", "cdeps", "ddeps", "kind", "dsem", "val", "inc", "lidx", "sig")


ENGS = ("pe", "act", "dve", "pool", "sp")


class Prog:
    def __init__(self):
        self.eng_ops = {e: [] for e in ENGS}
        self.lastw = {}
        self.readers = {}
        self.dsem_val = {}

    def add(self, eng, fn, reads=(), writes=(), kind="c", dsem=None, inc=16):
        op = Op()
        op.eng, op.fn, op.kind, op.dsem, op.inc = eng, fn, kind, dsem, inc
        op.lidx = len(self.eng_ops[eng])
        op.sig = False
        cdeps = {}
        ddeps = {}

        def dep(o, hazard):
            if o.kind == "c":
                if o.eng == eng and kind == "c":
                    if eng == "pe":
                        return
                cur = cdeps.get(o.eng)
                if cur is None or o.lidx > cur:
                    cdeps[o.eng] = o.lidx
            else:
                cur = ddeps.get(o.dsem)
                if cur is None or o.val > cur:
                    ddeps[o.dsem] = o.val
        lastw, readers = self.lastw, self.readers
        for k in reads:
            w = lastw.get(k)
            if w is not None:
                dep(w, True)
        for k in writes:
            w = lastw.get(k)
            if w is not None:
                dep(w, True)
            rl = readers.get(k)
            if rl:
                for o in rl.values():
                    dep(o, False)
        for k in reads:
            rl = readers.get(k)
            if rl is None:
                rl = readers[k] = {}
            rl[(eng, dsem) if kind != "c" else eng] = op
        for k in writes:
            lastw[k] = op
            readers[k] = {}
        if kind != "c":
            v = self.dsem_val.get(dsem, 0) + inc
            self.dsem_val[dsem] = v
            op.val = v
        op.cdeps, op.ddeps = cdeps, ddeps
        self.eng_ops[eng].append(op)
        return op

    def finalize(self):
        for e in ENGS:
            for op in self.eng_ops[e]:
                for de, li in op.cdeps.items():
                    self.eng_ops[de][li].sig = True
        self.sigval = {}
        for e in ENGS:
            cnt = 0
            vals = []
            for op in self.eng_ops[e]:
                if op.kind == "c" and op.sig:
                    cnt += 1
                vals.append(cnt)
            self.sigval[e] = vals

    def emit(self, e, engine, engsem, dsems):
        waited = {}
        for op in self.eng_ops[e]:
            for de, li in op.cdeps.items():
                v = self.sigval[de][li]
                key = ("e", de)
                if waited.get(key, 0) < v:
                    engine.wait_ge(engsem[de], v)
                    waited[key] = v
            for ds, v in op.ddeps.items():
                key = ("d", ds)
                if waited.get(key, 0) < v:
                    engine.wait_ge(dsems[ds], v)
                    waited[key] = v
            ins = op.fn(engine)
            if op.kind == "c":
                if op.sig:
                    ins.then_inc(engsem[e], 1)
            else:
                ins.then_inc(dsems[op.dsem], op.inc)


def build_nc():
    nc = bass.Bass("TRN2", target_bir_lowering=False)
    P = Prog()

    def dram(name, shape, kind="ExternalInput"):
        return nc.dram_tensor(name, list(shape), F32, kind=kind).ap()

    xh = dram("xh", [R, T + HALO, D])
    cvd = dram("cv", [128, NCV])
    gfd = dram("gfin", [128, D])
    w_in = dram("w_in", [D, 9216])
    w_pool = dram("w_pool", [1024, 256])
    w_a = dram("w_a", [2048, 256])
    w_i = dram("w_i", [2048, 256])
    w_pp = dram("w_pool_proj", [1024, D])
    w_lp = dram("w_lru_proj", [2048, D])
    w_out = dram("w_out", [D, D])
    w_up = dram("w_up", [D, 2 * DFF])
    w_down = dram("w_down", [DFF, D])
    outd = dram("out", [R, T, D], kind="ExternalOutput")
    ccin = [dram(f"ccin{u}", [128, 32], kind="Internal") for u in range(2 * R)]
    ccout = [dram(f"ccout{u}", [4 * 128, 32], kind="Internal") for u in range(2 * R)]

    es = ExitStack()
    with es:
        assert nc.sbuf_base <= BASE
        arena = es.enter_context(nc.sbuf_tensor("arena", [128, (229344 - BASE) // 4], F32))
        V = lambda name, fshape, dt, off: View(nc, name, fshape, dt, off)
        ylru = V("ylru", [16, T], BF16, OFF_A)
        ypool = V("ypool", [8, T], BF16, OFF_A + 32768)
        xres = V("xres", [NT, D], F32, OFF_A)
        hB = V("hB", [16, T], BF16, OFF_B)
        p2 = V("p2", [16, T], BF16, OFF_C)
        slots = [V(f"slot{s}", [SLOT_ELEMS], BF16, OFF_RING + s * 16384) for s in range(NSLOT)]
        gfin = V("gfin_sb", [D], F32, OFF_F)
        cv = V("cv_sb", [NCV], F32, OFF_F + 8192)
        drv = V("drv", [96], F32, OFF_F + 10304)
        ident = V("ident", [128], F32, OFF_F + 10688)
        sm = V("sm", [256], F32, OFF_F + 11200)
        hhalo = V("hhalo", [16, HALO], BF16, OFF_F + 12224)
        h2halo = V("h2halo", [16, 2], BF16, OFF_F + 12736)
        ccg = V("ccg1", [4, 32], F32, OFF_F + 12800)
        ccg2 = V("ccg2", [4, 32], F32, OFF_F + 13312)
        ident_bf = V("ident_bf", [128], BF16, OFF_F + 13824)
        OFF_T = OFF_F + 14080
        TSZ = 4224
        tf = [V(f"tf{i}", [1056], F32, OFF_T + i * TSZ) for i in range(8)]
        xns = [V("xn%d" % i, [D], BF16, OFF_T + (4 + i) * TSZ) for i in range(4)]
        vv = V("vv", [2, T], F32, OFF_A + 49152)
        vb = V("vb", [2, T], BF16, OFF_A + 49152 + 8192)
        dbuf = V("dbuf", [2, T], BF16, OFF_A + 49152 + 12288)
        gl1 = V("gl1", [T], F32, OFF_A + 49152 + 12288)
        dbufB = V("dbufB", [2, T], BF16, OFF_A + 49152 + 8192)
        dbufs = [dbuf, dbufB]
        wpool_sb = V("wpool_sb", [8, 256], BF16, OFF_A + 49152)
        ex = [V("ex0", [T], F32, OFF_A + 12 * 2048), V("ex1", [T], F32, OFF_A + 14 * 2048),
              V("ex2", [T], F32, OFF_C + 12 * 2048), V("ex3", [T], F32, OFF_C + 14 * 2048)]
        ost = [V("ost0", [D], F32, OFF_T), V("ost1", [D], F32, OFF_T + 2 * TSZ)]
        xst = V("xst", [2, D], F32, OFF_T)

        DV_HCL, DV_HBA, DV_HBI, DV_E, DV_SP, DV_TMP = 0, 16, 32, 48, 64, 80
        SM_SS, SM_SD, SM_RS = 0, 16, 32
        SM_CC1, SM_HIN, SM_SC, SM_CC2, SM_HH, SM_T16, SM_GC = 48, 80, 96, 112, 144, 176, 192
        SM_KEYS = tuple(("sm", i) for i in range(48)) + tuple(("sm", n) for n in ("cc1", "hin", "sc", "cc2", "hh", "t16", "gc"))

        ps = [es.enter_context(nc.psum_tensor(f"ps{b}", [128, 512], F32)) for b in range(8)]
        engsem = {e: es.enter_context(nc.semaphore(f"sem_{e}")) for e in ENGS}
        dsem_names = (["slot%d" % s for s in range(NSLOT)] + ["c0", "c1", "x0", "x1", "xs0", "xs1", "xs2", "xs3",
                      "o0", "o1", "cci", "ccip", "cc", "ccb", "wp"])
        dsems = {n: es.enter_context(nc.semaphore("ds_" + n)) for n in dsem_names}
        block = es.enter_context(nc.Block())

        bank_ctr = [0]

        held_banks = set()

        def nbank():
            while True:
                b = bank_ctr[0] % 8
                bank_ctr[0] += 1
                if b not in held_banks:
                    return b

        def PK(b):
            return (("ps", b),)

        def col(vw, c, n=1):
            return vw[:, c:c + n]

        P.add("sp", lambda e: e.dma_start(out=cv[:, :], in_=cvd), writes=cv.k(), kind="dma", dsem="c0")
        P.add("sp", lambda e: e.dma_start(out=gfin[:, :], in_=gfd), writes=gfin.k(), kind="dma", dsem="c1")

        P.add("pool", lambda e: e.memset(ident[:, :], 0.0), writes=ident.k())
        P.add("pool", lambda e: e.affine_select(out=ident[:, :], in_=ident[:, :], pattern=[[-1, 128]],
                                                compare_op=ALU.not_equal, fill=1.0, base=0, channel_multiplier=1),
              reads=ident.k(), writes=ident.k())

        P.add("dve", lambda e: e.tensor_copy(out=ident_bf[:, :], in_=ident[:, :]), reads=ident.k(), writes=ident_bf.k())
        kcv = cv.k()
        kdrv = drv.k()
        P.add("act", lambda e: e.activation(out=col(drv, DV_E, 16), in_=col(cv, CV_LAM, 16), func=AF.Exp, scale=-1.0),
              reads=kcv, writes=kdrv)
        P.add("dve", lambda e: e.tensor_scalar(out=col(drv, DV_TMP, 16), in0=col(drv, DV_E, 16), scalar1=-0.25,
                                               scalar2=1.0 / 3.0, op0=ALU.mult, op1=ALU.add), reads=kdrv, writes=kdrv)
        P.add("dve", lambda e: e.tensor_tensor(out=col(drv, DV_TMP, 16), in0=col(drv, DV_TMP, 16),
                                               in1=col(drv, DV_E, 16), op=ALU.mult), reads=kdrv, writes=kdrv)
        P.add("dve", lambda e: e.tensor_scalar(out=col(drv, DV_TMP, 16), in0=col(drv, DV_TMP, 16), scalar1=-1.0,
                                               scalar2=0.5, op0=ALU.mult, op1=ALU.add), reads=kdrv, writes=kdrv)
        P.add("dve", lambda e: e.tensor_tensor(out=col(drv, DV_TMP, 16), in0=col(drv, DV_TMP, 16),
                                               in1=col(drv, DV_E, 16), op=ALU.mult), reads=kdrv, writes=kdrv)
        P.add("dve", lambda e: e.tensor_scalar(out=col(drv, DV_TMP, 16), in0=col(drv, DV_TMP, 16), scalar1=-1.0,
                                               scalar2=1.0, op0=ALU.mult, op1=ALU.add), reads=kdrv, writes=kdrv)
        P.add("dve", lambda e: e.tensor_tensor(out=col(drv, DV_SP, 16), in0=col(drv, DV_TMP, 16),
                                               in1=col(drv, DV_E, 16), op=ALU.mult), reads=kdrv, writes=kdrv)
        P.add("dve", lambda e: e.tensor_scalar(out=col(drv, DV_HCL, 16), in0=col(drv, DV_SP, 16), scalar1=-4.0,
                                               scalar2=None, op0=ALU.mult), reads=kdrv, writes=kdrv)
        P.add("dve", lambda e: e.tensor_scalar(out=col(drv, DV_HBA, 16), in0=col(cv, CV_BA, 16), scalar1=0.5,
                                               scalar2=None, op0=ALU.mult), reads=kcv, writes=kdrv)
        P.add("dve", lambda e: e.tensor_scalar(out=col(drv, DV_HBI, 16), in0=col(cv, CV_BI, 16), scalar1=0.5,
                                               scalar2=None, op0=ALU.mult), reads=kcv, writes=kdrv)
        P.add("dve", lambda e: e.memset(sm[:, :], 0.0), writes=sm.k() + SM_KEYS)

        ring_ctr = [0]

        def load_unit(parts):
            s = ring_ctr[0] % NSLOT
            ring_ctr[0] += 1
            sl = slots[s]
            ops = []
            for i, (eo, src, kc_, n_) in enumerate(parts):
                def f(e, eo=eo, src=src, kc_=kc_, n_=n_, sl=sl):
                    dst = sl[:, eo:eo + kc_ * n_].rearrange("p (k n) -> p k n", n=n_)
                    return e.dma_start(out=dst, in_=src)
                ops.append(P.add("pool", f, writes=sl.k((eo, eo + kc_ * n_)), kind="dma", dsem="slot%d" % s))
            for o in ops:
                o.val = ops[-1].val
            return s

        def wsrc(w, r0, nk, c0, n):
            return w[r0:r0 + nk * 128, c0:c0 + n].rearrange("(k p) n -> p k n", p=128)

        def wv(s, eo, kc_, n_):
            return slots[s][:, eo:eo + kc_ * n_].rearrange("p (k n) -> p k n", n=n_)

        def mm_group(out_ap, pskeys, lhs_list, rhs_list, reads):
            def f(e):
                n = len(lhs_list)
                ins = None
                for i in range(n):
                    ins = e.matmul(out_ap, lhsT=lhs_list[i], rhs=rhs_list[i], start=(i == 0), stop=(i == n - 1))
                return ins
            return P.add("pe", f, reads=reads, writes=pskeys)

        rot = {"ss": 0, "x": 0, "xs": 0, "o": 0, "xn": 0}

        def norm_A(src_ap, src_keys, npart):
            sc = rot["ss"] % 16
            rot["ss"] += 1
            xn = xns[rot["xn"] % 4]
            rot["xn"] += 1
            kss, ksd, krs = (("sm", SM_SS + sc),), (("sm", SM_SD + sc),), (("sm", SM_RS + sc),)
            ss = sm[0:npart, SM_SS + sc:SM_SS + sc + 1]
            sd = sm[0:npart, SM_SD + sc:SM_SD + sc + 1]
            rs = sm[0:npart, SM_RS + sc:SM_RS + sc + 1]
            P.add("act", lambda e: e.activation(out=xn[0:npart, :], in_=src_ap, func=AF.Square, accum_out=ss),
                  reads=src_keys, writes=xn.k() + kss)
            P.add("act", lambda e: e.activation(out=sd, in_=ss, func=AF.Sqrt, scale=1.0 / D, bias=EPS),
                  reads=kss, writes=ksd)
            P.add("dve", lambda e: e.reciprocal(out=rs, in_=sd), reads=ksd, writes=krs)
            P.add("act", lambda e: e.activation(out=xn[0:npart, :], in_=src_ap, func=AF.Copy, scale=rs),
                  reads=src_keys + krs, writes=xn.k())
            return (xn, npart)

        def norm_B(ctx, gcol, dst_view, dst_col0, ncols):
            xn, npart = ctx
            for g4 in range(4):
                b = nbank()
                pv = ps[b][:, :].bitcast(BF16).rearrange("p (a t) -> p a t", t=128)[:, 0:4, :]

                def ft(e, g4=g4, pv=pv, xn=xn):
                    ins = None
                    for j in range(4):
                        c = g4 * 4 + j
                        ins = e.transpose(out=pv[:, j, 0:npart], in_=xn[0:npart, c * 128:(c + 1) * 128],
                                          identity=ident_bf[0:npart, 0:npart])
                    return ins
                P.add("pe", ft, reads=xn.k((g4 * 512, g4 * 512 + 512)) + ident_bf.k(), writes=PK(b))

                def fe(e, g4=g4, pv=pv):
                    gsl = cv[:, gcol + g4 * 4:gcol + g4 * 4 + 4]
                    return e.tensor_tensor(out=dst_view[:, g4 * 4:g4 * 4 + 4, dst_col0:dst_col0 + ncols],
                                           in0=pv[:, :, 0:npart],
                                           in1=gsl.unsqueeze(2).to_broadcast([128, 4, npart]), op=ALU.mult)
                P.add("dve", fe, reads=PK(b) + kcv,
                      writes=dst_view.k((g4 * 4, g4 * 4 + 4), (dst_col0, dst_col0 + ncols)))

        def norm_to_fm(src_ap, src_keys, npart, gcol, dst_view, dst_col0, ncols):
            norm_B(norm_A(src_ap, src_keys, npart), gcol, dst_view, dst_col0, ncols)

        cc_u = [0]

        pending_rb = []

        def exchange_readback():
            u, dst_view = pending_rb.pop()
            P.add("sp", lambda e: e.dma_start(out=dst_view[:, :, :], in_=ccout[u].rearrange("(c p) n -> p c n", p=128)),
                  reads=(("ccout", u),), writes=dst_view.k(), kind="dma", dsem="ccb")

        def exchange(src_ap, src_keys, dst_view, in_queue="sp", defer_readback=False):
            u = cc_u[0]
            cc_u[0] += 1
            kin, kout = (("ccin", u),), (("ccout", u),)
            P.add(in_queue, lambda e: e.dma_start(out=ccin[u], in_=src_ap), reads=src_keys, writes=kin,
                  kind="dma", dsem="cci" if in_queue == "sp" else "ccip")
            P.add("pool", lambda e: e.collective_compute("AllGather", ALU.bypass,
                                                         replica_groups=[[0, 1, 2, 3], [4, 5, 6, 7]],
                                                         ins=[ccin[u]], outs=[ccout[u]]),
                  reads=kin, writes=kout, kind="cc", dsem="cc", inc=1)
            pending_rb.append((u, dst_view))
            if not defer_readback:
                exchange_readback()

        st0_done = set()

        def st0_steps(r):
            state = {"prevB": None}
            steps = []
            for ti in range(-1, NT):
                def step(ti=ti):
                    par = rot["x"] % 2
                    rot["x"] += 1
                    if ti < 0:
                        npart, src = HALO, xh[r, 0:HALO, :]
                    else:
                        npart, src = 128, xh[r, HALO + ti * 128:HALO + (ti + 1) * 128, :]
                    P.add("sp", lambda e, par=par, npart=npart, src=src: e.dma_start(out=xst[0:npart, par, :], in_=src),
                          writes=xst.k(par), kind="dma", dsem="x%d" % par)
                    ctx = norm_A(xst[0:npart, par, :], xst.k(par), npart)
                    if state["prevB"] is not None:
                        norm_B(*state["prevB"])
                    state["prevB"] = (ctx, CV_GMIX, hhalo, 0, HALO) if ti < 0 else (ctx, CV_GMIX, hB, ti * 128, 128)
                steps.append(step)
            steps.append(lambda: norm_B(*state["prevB"]))
            return steps

        cur_round = [0]

        def final_A(tt):
            sc = rot["ss"] % 16
            rot["ss"] += 1
            kss, ksd = (("sm", SM_SS + sc),), (("sm", SM_SD + sc),)
            ss, sd = col(sm, SM_SS + sc), col(sm, SM_SD + sc)
            oi = rot["o"] % 2
            rot["o"] += 1
            ob = ost[oi]
            P.add("act", lambda e, tt=tt, ss=ss, ob=ob: e.activation(out=ob[:, :], in_=xres[:, tt, :], func=AF.Square,
                                                                     accum_out=ss), reads=xres.k(tt), writes=ob.k() + kss)
            P.add("act", lambda e, ss=ss, sd=sd: e.activation(out=sd, in_=ss, func=AF.Sqrt, scale=1.0 / D, bias=EPS),
                  reads=kss, writes=ksd)
            return (tt, sc, oi)

        def final_B(ctx):
            tt, sc, oi = ctx
            r = cur_round[0]
            ksd, krs = (("sm", SM_SD + sc),), (("sm", SM_RS + sc),)
            sd, rs = col(sm, SM_SD + sc), col(sm, SM_RS + sc)
            ob = ost[oi]
            P.add("dve", lambda e, sd=sd, rs=rs: e.reciprocal(out=rs, in_=sd), reads=ksd, writes=krs)
            P.add("dve", lambda e, tt=tt, rs=rs, ob=ob: e.scalar_tensor_tensor(
                out=ob[:, :], in0=xres[:, tt, :], scalar=rs, in1=gfin[:, :], op0=ALU.mult, op1=ALU.mult),
                reads=xres.k(tt) + krs + gfin.k(), writes=ob.k())
            P.add("sp", lambda e, tt=tt, ob=ob, r=r: e.dma_start(out=outd[r, tt * 128:(tt + 1) * 128, :], in_=ob[:, :]),
                  reads=ob.k(), writes=(("out", r, tt),), kind="dma", dsem="o%d" % oi)

        for r in range(R):
            cur_round[0] = r
            if r not in st0_done:
                for st in st0_steps(r):
                    st()

            zt = tf[7]
            P.add("dve", lambda e: e.memset(zt[:, 0:T], 0.0), writes=zt.k((0, T)))

            ul, Hb, Ab, gl0 = tf[0], tf[4], tf[5], tf[6]
            gls = [gl0, gl1]

            def lru_load(hb):
                sA = load_unit([(0, wsrc(w_in, 0, 16, 1024 + hb * 256, 256), 16, 256)])
                sB = load_unit([(0, wsrc(w_in, 0, 16, 3072 + hb * 256, 256), 16, 256),
                                (4096, wsrc(w_a, hb * 256, 2, 0, 256), 2, 256),
                                (4608, wsrc(w_i, hb * 256, 2, 0, 256), 2, 256)])
                if hb <= 5:
                    bsets = [(tf[1], tf[2], tf[3]), (ex[0], ex[1], ex[2])]
                elif hb == 6:
                    bsets = [(tf[1], tf[2], tf[3]), (ex[1], ex[3], tf[3])]
                else:
                    bsets = [(tf[1], tf[2], tf[3]), (tf[1], tf[2], tf[3])]
                return dict(hb=hb, wl=wv(sA, 0, 16, 256), wa=wv(sB, 4096, 2, 256), wi=wv(sB, 4608, 2, 256),
                            wg=wv(sB, 0, 16, 256), kA_l=slots[sA].k((0, 4096)), kA_g=slots[sB].k((4096, 5120)),
                            kB=slots[sB].k((0, 4096)), bsets=bsets,
                            gls=[gl0, ex[3] if (hb <= 5 and hb % 2 == 1) else gl1])

            def lru_X(cx, ci):
                hb, wl, kA_l = cx["hb"], cx["wl"], cx["kA_l"]
                c = 2 * hb + ci
                segs = [(None, HALO, 0)] + [(tb, TB, HALO + tb * TB) for tb in range(2)]
                for (tb, n, uoff) in segs:
                    b = nbank()
                    if tb is None:
                        rhs = [hhalo[:, k, :] for k in range(KC)]
                        rk = hhalo.k()
                    else:
                        rhs = [hB[:, k, tb * TB:(tb + 1) * TB] for k in range(KC)]
                        rk = hB.k(None, (tb * TB, (tb + 1) * TB))
                    mm_group(ps[b][:, 0:n], PK(b), [wl[:, k, ci * 128:(ci + 1) * 128] for k in range(KC)], rhs,
                             kA_l + rk)
                    P.add("act", lambda e, b=b, n=n, uoff=uoff: e.activation(out=ul[:, uoff:uoff + n],
                                                                             in_=ps[b][:, 0:n], func=AF.Copy),
                          reads=PK(b), writes=ul.k((uoff, uoff + n)))
                P.add("act", lambda e, c=c, ci=ci: e.activation(out=vv[:, ci, :], in_=ul[:, HALO:HALO + T],
                                                                func=AF.Identity,
                                                                scale=col(cv, CV_LCW + c * 4 + 3),
                                                                bias=col(cv, CV_LCB + c)),
                      reads=ul.k() + kcv, writes=vv.k(ci))
                for kk in range(3):
                    P.add("dve", lambda e, c=c, ci=ci, kk=kk: e.scalar_tensor_tensor(
                        out=vv[:, ci, :], in0=ul[:, HALO - 3 + kk:HALO - 3 + kk + T],
                        scalar=col(cv, CV_LCW + c * 4 + kk), in1=vv[:, ci, :], op0=ALU.mult, op1=ALU.add),
                        reads=ul.k() + kcv + vv.k(ci), writes=vv.k(ci))
                P.add("act", lambda e, ci=ci: e.activation(out=vb[:, ci, :], in_=vv[:, ci, :], func=AF.Copy),
                      reads=vv.k(ci), writes=vb.k(ci))

            def lru_G(cx):
                wg, kB = cx["wg"], cx["kB"]
                for ci in range(2):
                    gl = cx["gls"][ci]
                    for tb in range(2):
                        b = nbank()
                        mm_group(ps[b][:, :], PK(b), [wg[:, k, ci * 128:(ci + 1) * 128] for k in range(KC)],
                                 [hB[:, k, tb * TB:(tb + 1) * TB] for k in range(KC)],
                                 kB + hB.k(None, (tb * TB, (tb + 1) * TB)))
                        P.add("act", lambda e, b=b, tb=tb, gl=gl: e.activation(out=gl[:, tb * TB:(tb + 1) * TB],
                                                                               in_=ps[b][:, :], func=AF.Gelu_apprx_tanh),
                              reads=PK(b), writes=gl.k((tb * TB, (tb + 1) * TB)))

            def gates(cx, ci):
                hb, wa, wi, kA_g = cx["hb"], cx["wa"], cx["wi"], cx["kA_g"]
                c = 2 * hb + ci
                thr, thi, mt = cx["bsets"][ci]
                for (wmat, bcol, dstv) in ((wa, DV_HBA, thr), (wi, DV_HBI, thi)):
                    for tb in range(2):
                        b = nbank()
                        mm_group(ps[b][:, :], PK(b), [wmat[:, k, ci * 128:(ci + 1) * 128] for k in range(2)],
                                 [vb[:, k, tb * TB:(tb + 1) * TB] for k in range(2)],
                                 kA_g + vb.k(None, (tb * TB, (tb + 1) * TB)))
                        P.add("act", lambda e, b=b, tb=tb, dstv=dstv, bcol=bcol, c=c: e.activation(
                            out=dstv[:, tb * TB:(tb + 1) * TB], in_=ps[b][:, :], func=AF.Tanh, scale=0.5,
                            bias=col(drv, bcol + c)),
                            reads=PK(b) + kdrv, writes=dstv.k((tb * TB, (tb + 1) * TB)))

            def act_exp(cx, ci):
                c = 2 * cx["hb"] + ci
                thr, thi, mt = cx["bsets"][ci]
                kthr = thr.k((0, T))
                P.add("act", lambda e, c=c, thr=thr: e.activation(out=thr[:, 0:T], in_=thr[:, 0:T], func=AF.Exp,
                                                                  scale=col(drv, DV_HCL + c),
                                                                  bias=col(drv, DV_HCL + c)),
                      reads=kthr + kdrv, writes=kthr)

            def act_sq(cx, ci):
                thr, thi, mt = cx["bsets"][ci]
                P.add("act", lambda e, thr=thr, mt=mt: e.activation(out=mt[:, 0:T], in_=thr[:, 0:T], func=AF.Square),
                      reads=thr.k((0, T)), writes=mt.k((0, T)))

            def act_sqrt(cx, ci):
                thr, thi, mt = cx["bsets"][ci]
                P.add("act", lambda e, mt=mt: e.activation(out=mt[:, 0:T], in_=mt[:, 0:T], func=AF.Sqrt, scale=-0.25,
                                                           bias=0.25), reads=mt.k((0, T)), writes=mt.k((0, T)))

            def dve_chain(cx, ci, part):
                c = 2 * cx["hb"] + ci
                thr, thi, mt = cx["bsets"][ci]
                gl = cx["gls"][ci]
                kthr, kthi, kmt = thr.k((0, T)), thi.k((0, T)), mt.k((0, T))
                kH, kAb = Hb.k((0, T)), Ab.k((0, T))
                if part == "tail":
                    kgl = gl.k((0, T))
                    P.add("dve", lambda e, c=c, gl=gl: e.tensor_tensor(out=ylru[:, c, :], in0=Hb[:, 0:T], in1=gl[:, 0:T],
                                                                       op=ALU.mult), reads=kH + kgl, writes=ylru.k(c))
                    P.add("dve", lambda e, c=c, gl=gl: e.tensor_tensor(out=p2[:, c, :], in0=Ab[:, 0:T], in1=gl[:, 0:T],
                                                                       op=ALU.mult), reads=kAb + kgl, writes=p2.k(c))
                    kc1 = (("sm", "cc1"),)
                    P.add("dve", lambda e, c=c: e.tensor_copy(out=col(sm, SM_CC1 + 2 * c), in_=Hb[:, T - 1:T]),
                          reads=kH, writes=kc1)
                    P.add("dve", lambda e, c=c: e.tensor_copy(out=col(sm, SM_CC1 + 2 * c + 1), in_=Ab[:, T - 1:T]),
                          reads=kAb, writes=kc1)
                    return
                if part in ("head", "t"):
                    P.add("dve", lambda e, ci=ci, thi=thi: e.scalar_tensor_tensor(out=thi[:, 0:T], in0=thi[:, 0:T],
                                                                                  scalar=1.0, in1=vv[:, ci, :],
                                                                                  op0=ALU.add, op1=ALU.mult),
                          reads=kthi + vv.k(ci), writes=kthi)
                if part == "t":
                    return
                P.add("dve", lambda e, thi=thi, mt=mt: e.tensor_tensor(out=thi[:, 0:T], in0=thi[:, 0:T], in1=mt[:, 0:T],
                                                                       op=ALU.mult), reads=kthi + kmt, writes=kthi)
                P.add("dve", lambda e, thr=thr, thi=thi: e.tensor_tensor_scan(
                    out=Hb[:, 0:T], data0=thr[:, 0:T], data1=thi[:, 0:T], initial=0.0, op0=ALU.mult, op1=ALU.add),
                    reads=kthr + kthi, writes=kH)
                P.add("dve", lambda e, thr=thr: e.tensor_tensor_scan(
                    out=Ab[:, 0:T], data0=thr[:, 0:T], data1=zt[:, 0:T], initial=1.0, op0=ALU.mult, op1=ALU.add),
                    reads=kthr + zt.k((0, T)), writes=kAb)

            cx = lru_load(0)
            lru_X(cx, 0)
            lru_X(cx, 1)
            lru_G(cx)
            for hb in range(8):
                bsets = cx["bsets"]
                dbl_t = bsets[0][0] is not bsets[1][0]
                dbl_m = bsets[0][2] is not bsets[1][2]
                nxt_box = [None]

                def chain_and_next(ci):
                    if ci == 1 and hb + 1 < 8 and nxt_box[0] is not None:
                        dve_chain(cx, ci, "t")
                        lru_X(nxt_box[0], ci)
                        dve_chain(cx, ci, "mid")
                        dve_chain(cx, ci, "tail")
                    else:
                        dve_chain(cx, ci, "head")
                        tail_and_next(ci)

                def tail_and_next(ci):
                    if hb + 1 < 8:
                        if nxt_box[0] is None:
                            nxt_box[0] = lru_load(hb + 1)
                        lru_X(nxt_box[0], ci)
                    dve_chain(cx, ci, "tail")

                if dbl_t:
                    gates(cx, 0)
                    gates(cx, 1)
                    act_exp(cx, 0)
                    act_exp(cx, 1)
                    if dbl_m:
                        act_sq(cx, 0)
                        act_sq(cx, 1)
                        act_sqrt(cx, 0)
                        act_sqrt(cx, 1)
                        chain_and_next(0)
                        chain_and_next(1)
                    else:
                        act_sq(cx, 0)
                        act_sqrt(cx, 0)
                        chain_and_next(0)
                        act_sq(cx, 1)
                        act_sqrt(cx, 1)
                        chain_and_next(1)
                else:
                    for ci in range(2):
                        gates(cx, ci)
                        act_exp(cx, ci)
                        act_sq(cx, ci)
                        act_sqrt(cx, ci)
                        chain_and_next(ci)
                if nxt_box[0] is not None:
                    lru_G(nxt_box[0])
                    cx = nxt_box[0]

            merged = p2

            def merge_ctx(j):
                sM = load_unit([(0, wsrc(w_pp, 0, 8, j * 128, 128), 8, 128),
                                (1024, wsrc(w_lp, 0, 16, j * 128, 128), 16, 128),
                                (3072, wsrc(w_in, 0, 16, 5120 + j * 128, 128), 16, 128),
                                (5120, wsrc(w_in, 0, 16, 7168 + j * 128, 128), 16, 128)])
                if j == 0:
                    bufs = [(tf[3], tf[4], tf[7], tf[0]), (tf[5], tf[6], tf[1], tf[2])]
                else:
                    bufs = [(tf[0], tf[1], tf[2], tf[3]), (tf[4], tf[5], tf[6], tf[7])]
                return dict(j=j, wpp=wv(sM, 0, 8, 128), wlp=wv(sM, 1024, 16, 128), wg0=wv(sM, 3072, 16, 128),
                            wg1=wv(sM, 5120, 16, 128), kM=slots[sM].k((0, 7168)), bufs=bufs, banks={})

            def merge_g(mc, tb):
                j, kM = mc["j"], mc["kM"]
                tsl = slice(tb * TB, (tb + 1) * TB)
                tr = (tb * TB, (tb + 1) * TB)
                g0t, g1t, m0t, m1t = mc["bufs"][tb]
                for (wg, gt, boff) in ((mc["wg0"], g0t, 0), (mc["wg1"], g1t, 16)):
                    b = nbank()
                    mm_group(ps[b][:, :], PK(b), [wg[:, k, :] for k in range(16)], [hB[:, k, tsl] for k in range(16)],
                             kM + hB.k(None, tr))
                    P.add("act", lambda e, b=b, gt=gt, j=j, boff=boff: e.activation(
                        out=gt[:, 0:TB], in_=ps[b][:, :], func=AF.Sigmoid, bias=col(cv, CV_BGATE + boff + j)),
                        reads=PK(b) + kcv, writes=gt.k((0, TB)))

            def merge_P(mc, tb):
                kM = mc["kM"]
                tsl = slice(tb * TB, (tb + 1) * TB)
                tr = (tb * TB, (tb + 1) * TB)
                g0t, g1t, m0t, m1t = mc["bufs"][tb]
                bP = nbank()
                mm_group(ps[bP][:, :], PK(bP), [mc["wpp"][:, k, :] for k in range(8)], [ypool[:, k, tsl] for k in range(8)],
                         kM + ypool.k(None, tr))
                P.add("dve", lambda e, bP=bP, g0t=g0t, m0t=m0t: e.tensor_tensor(out=m0t[:, 0:TB], in0=g0t[:, 0:TB],
                                                                                in1=ps[bP][:, :], op=ALU.mult),
                      reads=PK(bP) + g0t.k((0, TB)), writes=m0t.k((0, TB)))

            def merge_p2(mc, tb):
                j, kM = mc["j"], mc["kM"]
                tsl = slice(tb * TB, (tb + 1) * TB)
                tr = (tb * TB, (tb + 1) * TB)
                g0t, g1t, m0t, m1t = mc["bufs"][tb]
                bL = nbank()
                mm_group(ps[bL][:, :], PK(bL), [mc["wlp"][:, k, :] for k in range(16)], [ylru[:, k, tsl] for k in range(16)],
                         kM + ylru.k(None, tr))
                P.add("dve", lambda e, bL=bL, g1t=g1t, m1t=m1t: e.tensor_tensor(out=m1t[:, 0:TB], in0=g1t[:, 0:TB],
                                                                                in1=ps[bL][:, :], op=ALU.mult),
                      reads=PK(bL) + g1t.k((0, TB)), writes=m1t.k((0, TB)))
                P.add("dve", lambda e, m0t=m0t, m1t=m1t, j=j, tsl=tsl: e.tensor_tensor(
                    out=merged[:, j, tsl], in0=m0t[:, 0:TB], in1=m1t[:, 0:TB], op=ALU.add),
                    reads=m0t.k((0, TB)) + m1t.k((0, TB)), writes=merged.k(j, tr))

            up, s1, s2 = tf[0], tf[1], tf[2]

            def pool_front(g):
                dbuf = dbufs[g % 2]
                sP = load_unit([(0, wsrc(w_in, 0, 16, g * 256, 256), 16, 256)])
                wu = wv(sP, 0, 16, 256)
                kP_u = slots[sP].k((0, 4096))
                for ci in range(2):
                    segs = [(None, HALO, 0)] + [(tb, TB, HALO + tb * TB) for tb in range(2)]
                    for (tb, n, uoff) in segs:
                        b = nbank()
                        if tb is None:
                            rhs = [hhalo[:, k, :] for k in range(KC)]
                            rk = hhalo.k()
                        else:
                            rhs = [hB[:, k, tb * TB:(tb + 1) * TB] for k in range(KC)]
                            rk = hB.k(None, (tb * TB, (tb + 1) * TB))
                        mm_group(ps[b][:, 0:n], PK(b), [wu[:, k, ci * 128:(ci + 1) * 128] for k in range(KC)], rhs,
                                 kP_u + rk)
                        P.add("act", lambda e, b=b, n=n, uoff=uoff: e.activation(out=up[:, uoff:uoff + n],
                                                                                 in_=ps[b][:, 0:n], func=AF.Copy),
                              reads=PK(b), writes=up.k((uoff, uoff + n)))
                    L = HALO + T
                    srcv, lo = up, 0
                    bufs = [s1, s2]
                    for lvl in range(g + 1):
                        sh = 1 << lvl
                        dstv = bufs[lvl % 2]
                        nlo = lo + sh
                        P.add("dve", lambda e, srcv=srcv, dstv=dstv, nlo=nlo, sh=sh, L=L: e.tensor_tensor(
                            out=dstv[:, nlo:L], in0=srcv[:, nlo:L], in1=srcv[:, nlo - sh:L - sh], op=ALU.add),
                            reads=srcv.k((lo, L)), writes=dstv.k((nlo, L)))
                        srcv, lo = dstv, nlo
                    Sv = srcv
                    w = POOL_W[g]
                    P.add("dve", lambda e, Sv=Sv, ci=ci, w=w, L=L: e.scalar_tensor_tensor(
                        out=dbuf[:, ci, :], in0=Sv[:, HALO:L], scalar=1.0 / w, in1=up[:, HALO:L], op0=ALU.mult,
                        op1=ALU.subtract), reads=Sv.k((HALO, L)) + up.k((HALO, L)), writes=dbuf.k(ci))
                    ktmp = (("sm", "t16"),)
                    P.add("dve", lambda e, Sv=Sv, g=g, r=r: e.tensor_tensor(
                        out=sm[:, SM_T16:SM_T16 + 16], in0=Sv[:, HALO:HALO + 16],
                        in1=cv[:, CV_INVC + (r * 4 + g) * 16:CV_INVC + (r * 4 + g) * 16 + 16], op=ALU.mult),
                        reads=Sv.k((HALO, HALO + 16)) + kcv, writes=ktmp)
                    P.add("dve", lambda e, ci=ci: e.tensor_tensor(out=dbuf[:, ci, 0:16], in0=sm[:, SM_T16:SM_T16 + 16],
                                                                  in1=up[:, HALO:HALO + 16], op=ALU.subtract),
                          reads=ktmp + up.k((HALO, HALO + 16)) + dbuf.k(ci), writes=dbuf.k(ci))
                return sP

            def pool_back(g, sP):
                dbuf = dbufs[g % 2]
                kP_p = wpool_sb.k((2 * g, 2 * g + 2))
                for m in range(2):
                    for tb in range(2):
                        b = nbank()
                        mm_group(ps[b][:, :], PK(b), [wpool_sb[:, 2 * g + k, m * 128:(m + 1) * 128] for k in range(2)],
                                 [dbuf[:, k, tb * TB:(tb + 1) * TB] for k in range(2)],
                                 kP_p + dbuf.k(None, (tb * TB, (tb + 1) * TB)))
                        P.add("act", lambda e, b=b, tb=tb, cc=2 * g + m: e.activation(
                            out=ypool[:, cc, tb * TB:(tb + 1) * TB], in_=ps[b][:, :], func=AF.Identity,
                            scale=col(cv, CV_PSCALE + cc)),
                            reads=PK(b) + kcv, writes=ypool.k(2 * g + m, (tb * TB, (tb + 1) * TB)))


            khin, ksc, kccg = (("sm", "hin"),), (("sm", "sc"),), ccg.k()
            hin = sm[:, SM_HIN:SM_HIN + 16]
            scar = sm[:, SM_SC:SM_SC + 16]

            def fold():
                P.add("dve", lambda e: e.memset(hin, 0.0), writes=khin)
                for j in range(4):
                    P.add("dve", lambda e, j=j: e.scalar_tensor_tensor(out=hin, in0=scar, scalar=col(cv, CV_SEL + j),
                                                                       in1=hin, op0=ALU.mult, op1=ALU.add),
                          reads=khin + ksc + kcv, writes=khin)
                    gj = ccg[:, j, :].rearrange("p (c two) -> p c two", two=2)
                    P.add("dve", lambda e, gj=gj: e.tensor_tensor(out=scar, in0=scar, in1=gj[:, :, 1], op=ALU.mult),
                          reads=ksc + kccg, writes=ksc)
                    P.add("dve", lambda e, gj=gj: e.tensor_tensor(out=scar, in0=scar, in1=gj[:, :, 0], op=ALU.add),
                          reads=ksc + kccg, writes=ksc)

            sPs = {}
            for g in range(4):
                if g == 2:
                    exchange(sm[:, SM_CC1:SM_CC1 + 32], (("sm", "cc1"),), ccg)
                sPs[g] = pool_front(g)
                if g == 1:
                    P.add("pool", lambda e: e.dma_start(out=wpool_sb[:, :, :],
                                                        in_=w_pool.rearrange("(k p) n -> p k n", p=128)),
                          writes=wpool_sb.k(), kind="dma", dsem="wp")
                if g >= 1:
                    pool_back(g - 1, sPs[g - 1])
            mc0 = merge_ctx(0)
            merge_g(mc0, 0)
            merge_g(mc0, 1)
            pool_back(3, sPs[3])
            fold()

            for tb in range(2):
                for c in range(16):
                    P.add("dve", lambda e, c=c, tb=tb: e.scalar_tensor_tensor(
                        out=ylru[:, c, tb * TB:(tb + 1) * TB], in0=p2[:, c, tb * TB:(tb + 1) * TB],
                        scalar=col(sm, SM_HIN + c), in1=ylru[:, c, tb * TB:(tb + 1) * TB], op0=ALU.mult, op1=ALU.add),
                        reads=p2.k(c, (tb * TB, (tb + 1) * TB)) + ylru.k(c, (tb * TB, (tb + 1) * TB)) + khin,
                        writes=ylru.k(c, (tb * TB, (tb + 1) * TB)))

            for j in range(16):
                if j == 0:
                    mc = mc0
                else:
                    mc = merge_ctx(j)
                if j == 0:
                    merge_P(mc, 0)
                    merge_P(mc, 1)
                    merge_p2(mc, 0)
                    merge_p2(mc, 1)
                else:
                    for tb in range(2):
                        merge_g(mc, tb)
                        merge_P(mc, tb)
                        merge_p2(mc, tb)

            tt_order = [7, 0, 1, 2, 3, 4, 5, 6]
            kc2 = (("sm", "cc2"),)
            cc2 = sm[:, SM_CC2:SM_CC2 + 32]
            pre_ffn = {}
            for cb in range(4):
                sO = load_unit([(0, wsrc(w_out, 0, 16, cb * 512, 512), 16, 512)])
                wo = wv(sO, 0, 16, 512)
                kO = slots[sO].k()
                if cb == 3:
                    pre_ffn[(0, 0)] = load_unit([(0, wsrc(w_up, 0, 16, 0, 256), 16, 256),
                                                 (4096, wsrc(w_up, 0, 16, DFF, 256), 16, 256)])
                ctxs = {}
                for i, tt in enumerate(tt_order):
                    if cb == 3 and i >= 1:
                        tA = tt_order[i - 1]
                        ctxs[tA] = norm_A(xres[:, tA, :], xres.k(tA), 128)
                    xi = rot["xs"] % 4
                    rot["xs"] += 1
                    xsb = tf[xi]
                    kxs = xsb.k((0, TB))
                    P.add("sp", lambda e, xsb=xsb, tt=tt, cb=cb, r=r: e.dma_start(
                        out=xsb[:, 0:TB], in_=xh[r, HALO + tt * 128:HALO + (tt + 1) * 128, cb * 512:(cb + 1) * 512]),
                        writes=kxs, kind="dma", dsem="xs%d" % xi)
                    b = nbank()
                    tr = (tt * 128, (tt + 1) * 128)
                    mm_group(ps[b][:, :], PK(b), [merged[:, k, tt * 128:(tt + 1) * 128] for k in range(KC)],
                             [wo[:, k, :] for k in range(KC)], kO + merged.k(None, tr))
                    P.add("dve", lambda e, b=b, xsb=xsb, tt=tt, cb=cb: e.tensor_tensor(
                        out=xres[:, tt, cb * 512:(cb + 1) * 512], in0=ps[b][:, :], in1=xsb[:, 0:TB], op=ALU.add),
                        reads=PK(b) + kxs, writes=xres.k(tt, (cb * 512, (cb + 1) * 512)))
                    if cb == 3 and i >= 2:
                        t2 = tt_order[i - 2]
                        norm_B(ctxs[t2], CV_GMLP, hB, t2 * 128, 128)
                        if t2 == 7:
                            P.add("dve", lambda e: e.tensor_copy(out=cc2.rearrange("p (c two) -> p c two", two=2),
                                                                 in_=hB[:, :, T - 2:T]),
                                  reads=hB.k(None, (T - 2, T)), writes=kc2)
                            exchange(cc2, kc2, ccg2, in_queue="pool", defer_readback=True)
            t6, t5 = tt_order[-1], tt_order[-2]
            norm_B(ctxs[t5], CV_GMLP, hB, t5 * 128, 128)
            norm_to_fm(xres[:, t6, :], xres.k(t6), 128, CV_GMLP, hB, t6 * 128, 128)
            exchange_readback()

            khh, kgc, kccg2 = (("sm", "hh"),), (("sm", "gc"),), ccg2.k()
            hh = sm[:, SM_HH:SM_HH + 32]
            gcar = sm[:, SM_GC:SM_GC + 32]
            P.add("dve", lambda e: e.tensor_scalar(out=hh, in0=gcar, scalar1=col(cv, CV_SEL + 0), scalar2=None,
                                                   op0=ALU.mult), reads=kgc + kcv, writes=khh)
            for j in range(1, 4):
                P.add("dve", lambda e, j=j: e.scalar_tensor_tensor(out=hh, in0=ccg2[:, j - 1, :],
                                                                   scalar=col(cv, CV_SEL + j), in1=hh,
                                                                   op0=ALU.mult, op1=ALU.add),
                      reads=khh + kccg2 + kcv, writes=khh)
            P.add("dve", lambda e: e.tensor_copy(out=h2halo[:, :, :], in_=hh.rearrange("p (c two) -> p c two", two=2)),
                  reads=khh, writes=h2halo.k())
            P.add("dve", lambda e: e.tensor_copy(out=gcar, in_=ccg2[:, 3, :]), reads=kccg2, writes=kgc)

            gv = p2
            def pair_ctx(grp, pr, alt=False):
                f0 = grp * 16 + 2 * pr
                if (grp, pr) in pre_ffn:
                    sU = pre_ffn[(grp, pr)]
                else:
                    sU = load_unit([(0, wsrc(w_up, 0, 16, f0 * 128, 256), 16, 256),
                                    (4096, wsrc(w_up, 0, 16, DFF + f0 * 128, 256), 16, 256)])
                wgt, wvl = wv(sU, 0, 16, 256), wv(sU, 4096, 16, 256)
                kU = slots[sU].k()
                res = []
                for fi in range(2):
                    f = f0 + fi
                    o = 2 * (f % 2)
                    base = 0 if alt else 4
                    res.append(dict(f=f, fl=f - grp * 16, gp=tf[base + o], cvb=tf[base + 1 + o], kU=kU,
                                    lw=[wgt[:, k, fi * 128:(fi + 1) * 128] for k in range(KC)],
                                    lv=[wvl[:, k, fi * 128:(fi + 1) * 128] for k in range(KC)]))
                return res

            def ffn_gate(c):
                gp, lw, kU = c["gp"], c["lw"], c["kU"]
                for tb in range(2):
                    b = nbank()
                    mm_group(ps[b][:, :], PK(b), lw, [hB[:, k, tb * TB:(tb + 1) * TB] for k in range(KC)],
                             kU + hB.k(None, (tb * TB, (tb + 1) * TB)))
                    P.add("act", lambda e, b=b, gp=gp, tb=tb: e.activation(
                        out=gp[:, 2 + tb * TB:2 + (tb + 1) * TB], in_=ps[b][:, :], func=AF.Copy),
                        reads=PK(b), writes=gp.k((2 + tb * TB, 2 + (tb + 1) * TB)))

            def ffn_halo(c):
                gp, lw, kU = c["gp"], c["lw"], c["kU"]
                b = nbank()
                mm_group(ps[b][:, 0:2], PK(b), lw, [h2halo[:, k, :] for k in range(KC)], kU + h2halo.k())
                P.add("act", lambda e, b=b, gp=gp: e.activation(out=gp[:, 0:2], in_=ps[b][:, 0:2], func=AF.Copy),
                      reads=PK(b), writes=gp.k((0, 2)))

            def ffn_post(c):
                gp, cvb, f = c["gp"], c["cvb"], c["f"]
                kgp, kcb = gp.k((0, T + 2)), cvb.k((0, T))
                P.add("act", lambda e, gp=gp, cvb=cvb, f=f: e.activation(
                    out=cvb[:, 0:T], in_=gp[:, 2:2 + T], func=AF.Identity, scale=col(cv, CV_FCW + f * 3 + 2),
                    bias=col(cv, CV_FCB + f)), reads=kgp + kcv, writes=kcb)
                for kk in range(2):
                    P.add("dve", lambda e, gp=gp, cvb=cvb, f=f, kk=kk: e.scalar_tensor_tensor(
                        out=cvb[:, 0:T], in0=gp[:, kk:kk + T], scalar=col(cv, CV_FCW + f * 3 + kk),
                        in1=cvb[:, 0:T], op0=ALU.mult, op1=ALU.add), reads=kgp + kcb + kcv, writes=kcb)
                P.add("act", lambda e, cvb=cvb: e.activation(out=cvb[:, 0:T], in_=cvb[:, 0:T],
                                                             func=AF.Gelu_apprx_tanh), reads=kcb, writes=kcb)

            def ffn_valmm(c):
                banks = []
                for tb in range(2):
                    b = nbank()
                    mm_group(ps[b][:, :], PK(b), c["lv"], [hB[:, k, tb * TB:(tb + 1) * TB] for k in range(KC)],
                             c["kU"] + hB.k(None, (tb * TB, (tb + 1) * TB)))
                    banks.append(b)
                    held_banks.add(b)
                return banks

            def ffn_mult(c, banks):
                cvb, fl = c["cvb"], c["fl"]
                for tb in range(2):
                    b = banks[tb]
                    held_banks.discard(b)
                    P.add("dve", lambda e, b=b, cvb=cvb, fl=fl, tb=tb: e.tensor_tensor(
                        out=gv[:, fl, tb * TB:(tb + 1) * TB], in0=cvb[:, tb * TB:(tb + 1) * TB],
                        in1=ps[b][:, :], op=ALU.mult),
                        reads=PK(b) + cvb.k((tb * TB, (tb + 1) * TB)),
                        writes=gv.k(fl, (tb * TB, (tb + 1) * TB)))

            for grp in range(3):
                pr_start = 0
                if grp == 0:
                    c0, c1 = pair_ctx(0, 0)
                    c2, c3 = pair_ctx(0, 1, alt=True)
                    ffn_gate(c0)
                    bk0 = ffn_valmm(c0)
                    ffn_gate(c1)
                    bk1 = ffn_valmm(c1)
                    ffn_gate(c2)
                    ffn_gate(c3)
                    ffn_halo(c0)
                    ffn_post(c0)
                    ffn_mult(c0, bk0)
                    ffn_halo(c1)
                    ffn_post(c1)
                    ffn_mult(c1, bk1)
                    for c in (c2, c3):
                        ffn_halo(c)
                        ffn_post(c)
                        ffn_mult(c, ffn_valmm(c))
                    pr_start = 2
                for pr in range(pr_start, 8):
                    for c in pair_ctx(grp, pr):
                        ffn_gate(c)
                        ffn_halo(c)
                        ffn_post(c)
                        ffn_mult(c, ffn_valmm(c))
                last = (grp == 2)
                nsteps = st0_steps(r + 1) if (last and r + 1 < R) else []
                for cb in range(4):
                    sD = load_unit([(0, wsrc(w_down, grp * 2048, 16, cb * 512, 512), 16, 512)])
                    wd = wv(sD, 0, 16, 512)
                    kD = slots[sD].k()
                    fuse = last and cb == 3
                    order = [6, 7, 0, 1, 2, 3, 4, 5] if fuse else list(range(NT))
                    prev = None
                    for tt in order:
                        b = nbank()
                        tr = (tt * 128, (tt + 1) * 128)
                        mm_group(ps[b][:, :], PK(b), [gv[:, k, tt * 128:(tt + 1) * 128] for k in range(KC)],
                                 [wd[:, k, :] for k in range(KC)], kD + gv.k(None, tr))
                        kx = xres.k(tt, (cb * 512, (cb + 1) * 512))
                        P.add("dve", lambda e, b=b, tt=tt, cb=cb: e.tensor_tensor(
                            out=xres[:, tt, cb * 512:(cb + 1) * 512], in0=ps[b][:, :],
                            in1=xres[:, tt, cb * 512:(cb + 1) * 512], op=ALU.add), reads=PK(b) + kx, writes=kx)
                        if fuse:
                            ctx = final_A(tt)
                            if prev is not None:
                                final_B(prev)
                            prev = ctx
                        elif nsteps and tt % 2 == 0:
                            nsteps.pop(0)()
                    if last and cb == 2:
                        while nsteps:
                            nsteps.pop(0)()
                        if r + 1 < R:
                            st0_done.add(r + 1)
                    if fuse:
                        final_B(prev)

        fin_o0, fin_o1 = P.dsem_val["o0"], P.dsem_val["o1"]

        def fin(e):
            e.wait_ge(dsems["o0"], fin_o0)
            e.wait_ge(dsems["o1"], fin_o1)
            return e.nop()
        P.add("sp", fin)

        P.finalize()

        @block.tensor
        def _(e):
            P.emit("pe", e, engsem, dsems)

        @block.scalar
        def _(e):
            P.emit("act", e, engsem, dsems)

        @block.vector
        def _(e):
            P.emit("dve", e, engsem, dsems)

        @block.gpsimd
        def _(e):
            P.emit("pool", e, engsem, dsems)

        @block.sync
        def _(e):
            P.emit("sp", e, engsem, dsems)
    return nc


def _fm(v, nch):
    return np.ascontiguousarray(np.asarray(v, np.float32).reshape(nch, 128).T)


_NC_CACHE = {}


def kernel(x, g_mix, w_in, b_gate, w_pool, pool_scale, lru_conv_w, lru_conv_b, w_a, b_a, w_i, b_i, lru_lambda,
           w_pool_proj, w_lru_proj, w_out, g_mlp, w_up, ffn_conv_w, ffn_conv_b, w_down, g_final):
    x = np.asarray(x, np.float32)
    B, S, _ = x.shape
    f = lambda a: np.ascontiguousarray(np.asarray(a, np.float32))
    cvb = np.zeros((128, NCV), np.float32)
    cvb[:, CV_GMIX:CV_GMIX + 16] = _fm(g_mix[0], 16)
    cvb[:, CV_GMLP:CV_GMLP + 16] = _fm(g_mlp[0], 16)
    cvb[:, CV_BGATE:CV_BGATE + 32] = _fm(b_gate[0], 32)
    cvb[:, CV_PSCALE:CV_PSCALE + 8] = _fm(pool_scale[0], 8)
    lcw = np.asarray(lru_conv_w[0], np.float32)
    cvb[:, CV_LCW:CV_LCW + 64] = lcw.reshape(4, 16, 128).transpose(2, 1, 0).reshape(128, 64)
    cvb[:, CV_LCB:CV_LCB + 16] = _fm(lru_conv_b[0], 16)
    cvb[:, CV_BA:CV_BA + 16] = _fm(b_a[0], 16)
    cvb[:, CV_BI:CV_BI + 16] = _fm(b_i[0], 16)
    cvb[:, CV_LAM:CV_LAM + 16] = _fm(lru_lambda[0], 16)
    fcw = np.asarray(ffn_conv_w[0], np.float32)
    cvb[:, CV_FCW:CV_FCW + 144] = fcw.reshape(3, 48, 128).transpose(2, 1, 0).reshape(128, 144)
    cvb[:, CV_FCB:CV_FCB + 48] = _fm(ffn_conv_b[0], 48)
    gfb = np.ascontiguousarray(np.broadcast_to(np.asarray(g_final, np.float32)[None, :], (128, D)))
    shared = {
        "gfin": gfb, "w_in": f(w_in[0]), "w_pool": f(w_pool[0]).reshape(1024, 256),
        "w_a": f(w_a[0]).reshape(2048, 256), "w_i": f(w_i[0]).reshape(2048, 256),
        "w_pool_proj": f(w_pool_proj[0]), "w_lru_proj": f(w_lru_proj[0]), "w_out": f(w_out[0]),
        "w_up": f(w_up[0]), "w_down": f(w_down[0]),
    }
    in_maps = []
    for c in range(NCORES):
        b, q = c // 4, c % 4
        xhc = np.zeros((R, T + HALO, D), np.float32)
        cvc = cvb.copy()
        cvc[:, CV_SEL + q] = 1.0
        for r in range(R):
            p = 4 * r + q
            t0 = p * T
            xhc[r, HALO:] = x[b, t0:t0 + T]
            if t0 > 0:
                xhc[r, :HALO] = x[b, t0 - HALO:t0]
            for g, w in enumerate(POOL_W):
                cnt = np.minimum(np.arange(t0 + 1, t0 + 17), w).astype(np.float32)
                cvc[:, CV_INVC + (r * 4 + g) * 16:CV_INVC + (r * 4 + g) * 16 + 16] = (1.0 / cnt)[None, :]
        m = dict(shared)
        m["xh"] = xhc
        m["cv"] = cvc
        in_maps.append(m)
    if "nc" not in _NC_CACHE:
        _NC_CACHE["nc"] = build_nc()
    nc = _NC_CACHE["nc"]
    res = run_bass_kernel_spmd(nc, in_maps, core_ids=list(range(NCORES)))
    out = np.empty((B, S, D), np.float32)
    for c in range(NCORES):
        b, q = c // 4, c % 4
        oc = np.asarray(res.results[c]["out"]).reshape(R, T, D)
        for r in range(R):
            p = 4 * r + q
            out[b, p * T:(p + 1) * T] = oc[r]
    return out
```

```python
import numpy as np
from contextlib import ExitStack
import concourse.bass as bass
import concourse.mybir as mybir
from concourse.bass_utils import run_bass_kernel_spmd

F32 = mybir.dt.float32
BF16 = mybir.dt.bfloat16
AF = mybir.ActivationFunctionType
ALU = mybir.AluOpType

NCORES = 8
R = 2
T = 1024
HALO = 16
D = 2048
KC = 16
NT = 8
TB = 512
DFF = 6144
EPS = 1e-6
POOL_W = (2, 4, 8, 16)

CV_GMIX, CV_GMLP, CV_BGATE, CV_PSCALE, CV_LCW, CV_LCB, CV_BA, CV_BI, CV_LAM, CV_FCW, CV_FCB, CV_SEL, CV_INVC = (
    0, 16, 32, 64, 72, 136, 152, 168, 184, 200, 344, 392, 396)
NCV = 396 + R * 4 * 16

BASE = 16512
OFF_A, OFF_B, OFF_C, OFF_RING, OFF_F = 0, 65536, 98304, 131072, 163840
NSLOT = 2
SLOT_ELEMS = 8192
UNIT = 256


class View:
    def __init__(self, nc, name, fshape, dt, off):
        self.es = 2 if dt == BF16 else 4
        self.t = nc.alloc_sbuf_tensor_at(name, [128] + list(fshape), dt, offset=BASE + off)
        self.off = off
        self.fshape = tuple(fshape)
        st = [1] * len(fshape)
        for i in range(len(fshape) - 2, -1, -1):
            st[i] = st[i + 1] * fshape[i + 1]
        self.st = st
        self._cache = {}
        assert off + self.es * int(np.prod(fshape)) <= 212832, name

    def __getitem__(self, idx):
        return self.t[idx]

    def k(self, *idx):
        if idx in self._cache:
            return self._cache[idx]
        rg = []
        for d, n in enumerate(self.fshape):
            i = idx[d] if d < len(idx) else None
            if i is None:
                rg.append((0, n))
            elif isinstance(i, int):
                rg.append((i, i + 1))
            else:
                rg.append(i)
        keys = set()

        def rec(d, base):
            if d == len(rg) - 1:
                b0 = self.off + (base + rg[d][0]) * self.es
                b1 = self.off + (base + rg[d][1]) * self.es
                keys.update(range(b0 // UNIT, (b1 + UNIT - 1) // UNIT))
            else:
                for i in range(rg[d][0], rg[d][1]):
                    rec(d + 1, base + i * self.st[d])
        rec(0, 0)
        res = tuple(keys)
        self._cache[idx] = res
        return res


class Op:
    __slots__ = ("eng", "fn", "cdeps", "ddeps", "kind", "dsem", "val", "inc", "lidx", "sig")


ENGS = ("pe", "act", "dve", "pool", "sp")


class Prog:
    def __init__(self):
        self.eng_ops = {e: [] for e in ENGS}
        self.lastw = {}
        self.readers = {}
        self.dsem_val = {}

    def add(self, eng, fn, reads=(), writes=(), kind="c", dsem=None, inc=16):
        op = Op()
        op.eng, op.fn, op.kind, op.dsem, op.inc = eng, fn, kind, dsem, inc
        op.lidx = len(self.eng_ops[eng])
        op.sig = False
        cdeps = {}
        ddeps = {}

        def dep(o, hazard):
            if o.kind == "c":
                if o.eng == eng and kind == "c":
                    if eng == "pe":
                        return
                cur = cdeps.get(o.eng)
                if cur is None or o.lidx > cur:
                    cdeps[o.eng] = o.lidx
            else:
                cur = ddeps.get(o.dsem)
                if cur is None or o.val > cur:
                    ddeps[o.dsem] = o.val
        lastw, readers = self.lastw, self.readers
        for k in reads:
            w = lastw.get(k)
            if w is not None:
                dep(w, True)
        for k in writes:
            w = lastw.get(k)
            if w is not None:
                dep(w, True)
            rl = readers.get(k)
            if rl:
                for o in rl.values():
                    dep(o, False)
        for k in reads:
            rl = readers.get(k)
            if rl is None:
                rl = readers[k] = {}
            rl[(eng, dsem) if kind != "c" else eng] = op
        for k in writes:
            lastw[k] = op
            readers[k] = {}
        if kind != "c":
            v = self.dsem_val.get(dsem, 0) + inc
            self.dsem_val[dsem] = v
            op.val = v
        op.cdeps, op.ddeps = cdeps, ddeps
        self.eng_ops[eng].append(op)
        return op

    def finalize(self):
        for e in ENGS:
            for op in self.eng_ops[e]:
                for de, li in op.cdeps.items():
                    self.eng_ops[de][li].sig = True
        self.sigval = {}
        for e in ENGS:
            cnt = 0
            vals = []
            for op in self.eng_ops[e]:
                if op.kind == "c" and op.sig:
                    cnt += 1
                vals.append(cnt)
            self.sigval[e] = vals

    def emit(self, e, engine, engsem, dsems):
        waited = {}
        for op in self.eng_ops[e]:
            for de, li in op.cdeps.items():
                v = self.sigval[de][li]
                key = ("e", de)
                if waited.get(key, 0) < v:
                    engine.wait_ge(engsem[de], v)
                    waited[key] = v
            for ds, v in op.ddeps.items():
                key = ("d", ds)
                if waited.get(key, 0) < v:
                    engine.wait_ge(dsems[ds], v)
                    waited[key] = v
            ins = op.fn(engine)
            if op.kind == "c":
                if op.sig:
                    ins.then_inc(engsem[e], 1)
            else:
                ins.then_inc(dsems[op.dsem], op.inc)


def build_nc():
    nc = bass.Bass("TRN2", target_bir_lowering=False)
    P = Prog()

    def dram(name, shape, kind="ExternalInput"):
        return nc.dram_tensor(name, list(shape), F32, kind=kind).ap()

    xh = dram("xh", [R, T + HALO, D])
    cvd = dram("cv", [128, NCV])
    gfd = dram("gfin", [128, D])
    w_in = dram("w_in", [D, 9216])
    w_pool = dram("w_pool", [1024, 256])
    w_a = dram("w_a", [2048, 256])
    w_i = dram("w_i", [2048, 256])
    w_pp = dram("w_pool_proj", [1024, D])
    w_lp = dram("w_lru_proj", [2048, D])
    w_out = dram("w_out", [D, D])
    w_up = dram("w_up", [D, 2 * DFF])
    w_down = dram("w_down", [DFF, D])
    outd = dram("out", [R, T, D], kind="ExternalOutput")
    ccin = [dram(f"ccin{u}", [128, 32], kind="Internal") for u in range(2 * R)]
    ccout = [dram(f"ccout{u}", [4 * 128, 32], kind="Internal") for u in range(2 * R)]

    es = ExitStack()
    with es:
        assert nc.sbuf_base <= BASE
        arena = es.enter_context(nc.sbuf_tensor("arena", [128, (229344 - BASE) // 4], F32))
        V = lambda name, fshape, dt, off: View(nc, name, fshape, dt, off)
        ylru = V("ylru", [16, T], BF16, OFF_A)
        ypool = V("ypool", [8, T], BF16, OFF_A + 32768)
        xres = V("xres", [NT, D], F32, OFF_A)
        hB = V("hB", [16, T], BF16, OFF_B)
        p2 = V("p2", [16, T], BF16, OFF_C)
        slots = [V(f"slot{s}", [SLOT_ELEMS], BF16, OFF_RING + s * 16384) for s in range(NSLOT)]
        gfin = V("gfin_sb", [D], F32, OFF_F)
        cv = V("cv_sb", [NCV], F32, OFF_F + 8192)
        drv = V("drv", [96], F32, OFF_F + 10304)
        ident = V("ident", [128], F32, OFF_F + 10688)
        sm = V("sm", [256], F32, OFF_F + 11200)
        hhalo = V("hhalo", [16, HALO], BF16, OFF_F + 12224)
        h2halo = V("h2halo", [16, 2], BF16, OFF_F + 12736)
        ccg = V("ccg1", [4, 32], F32, OFF_F + 12800)
        ccg2 = V("ccg2", [4, 32], F32, OFF_F + 13312)
        ident_bf = V("ident_bf", [128], BF16, OFF_F + 13824)
        OFF_T = OFF_F + 14080
        TSZ = 4224
        tf = [V(f"tf{i}", [1056], F32, OFF_T + i * TSZ) for i in range(8)]
        xns = [V("xn%d" % i, [D], BF16, OFF_T + (4 + i) * TSZ) for i in range(4)]
        vv = V("vv", [2, T], F32, OFF_A + 49152)
        vb = V("vb", [2, T], BF16, OFF_A + 49152 + 8192)
        dbuf = V("dbuf", [2, T], BF16, OFF_A + 49152 + 12288)
        gl1 = V("gl1", [T], F32, OFF_A + 49152 + 12288)
        dbufB = V("dbufB", [2, T], BF16, OFF_A + 49152 + 8192)
        dbufs = [dbuf, dbufB]
        wpool_sb = V("wpool_sb", [8, 256], BF16, OFF_A + 49152)
        ex = [V("ex0", [T], F32, OFF_A + 12 * 2048), V("ex1", [T], F32, OFF_A + 14 * 2048),
              V("ex2", [T], F32, OFF_C + 12 * 2048), V("ex3", [T], F32, OFF_C + 14 * 2048)]
        ost = [V("ost0", [D], F32, OFF_T), V("ost1", [D], F32, OFF_T + 2 * TSZ)]
        xst = V("xst", [2, D], F32, OFF_T)

        DV_HCL, DV_HBA, DV_HBI, DV_E, DV_SP, DV_TMP = 0, 16, 32, 48, 64, 80
        SM_SS, SM_SD, SM_RS = 0, 16, 32
        SM_CC1, SM_HIN, SM_SC, SM_CC2, SM_HH, SM_T16, SM_GC = 48, 80, 96, 112, 144, 176, 192
        SM_KEYS = tuple(("sm", i) for i in range(48)) + tuple(("sm", n) for n in ("cc1", "hin", "sc", "cc2", "hh", "t16", "gc"))

        ps = [es.enter_context(nc.psum_tensor(f"ps{b}", [128, 512], F32)) for b in range(8)]
        engsem = {e: es.enter_context(nc.semaphore(f"sem_{e}")) for e in ENGS}
        dsem_names = (["slot%d" % s for s in range(NSLOT)] + ["c0", "c1", "x0", "x1", "xs0", "xs1", "xs2", "xs3",
                      "o0", "o1", "cci", "ccip", "cc", "ccb", "wp"])
        dsems = {n: es.enter_context(nc.semaphore("ds_" + n)) for n in dsem_names}
        block = es.enter_context(nc.Block())

        bank_ctr = [0]

        held_banks = set()

        def nbank():
            while True:
                b = bank_ctr[0] % 8
                bank_ctr[0] += 1
                if b not in held_banks:
                    return b

        def PK(b):
            return (("ps", b),)

        def col(vw, c, n=1):
            return vw[:, c:c + n]

        P.add("sp", lambda e: e.dma_start(out=cv[:, :], in_=cvd), writes=cv.k(), kind="dma", dsem="c0")
        P.add("sp", lambda e: e.dma_start(out=gfin[:, :], in_=gfd), writes=gfin.k(), kind="dma", dsem="c1")

        P.add("pool", lambda e: e.memset(ident[:, :], 0.0), writes=ident.k())
        P.add("pool", lambda e: e.affine_select(out=ident[:, :], in_=ident[:, :], pattern=[[-1, 128]],
                                                compare_op=ALU.not_equal, fill=1.0, base=0, channel_multiplier=1),
              reads=ident.k(), writes=ident.k())

        P.add("dve", lambda e: e.tensor_copy(out=ident_bf[:, :], in_=ident[:, :]), reads=ident.k(), writes=ident_bf.k())
        kcv = cv.k()
        kdrv = drv.k()
        P.add("act", lambda e: e.activation(out=col(drv, DV_E, 16), in_=col(cv, CV_LAM, 16), func=AF.Exp, scale=-1.0),
              reads=kcv, writes=kdrv)
        P.add("dve", lambda e: e.tensor_scalar(out=col(drv, DV_TMP, 16), in0=col(drv, DV_E, 16), scalar1=-0.25,
                                               scalar2=1.0 / 3.0, op0=ALU.mult, op1=ALU.add), reads=kdrv, writes=kdrv)
        P.add("dve", lambda e: e.tensor_tensor(out=col(drv, DV_TMP, 16), in0=col(drv, DV_TMP, 16),
                                               in1=col(drv, DV_E, 16), op=ALU.mult), reads=kdrv, writes=kdrv)
        P.add("dve", lambda e: e.tensor_scalar(out=col(drv, DV_TMP, 16), in0=col(drv, DV_TMP, 16), scalar1=-1.0,
                                               scalar2=0.5, op0=ALU.mult, op1=ALU.add), reads=kdrv, writes=kdrv)
        P.add("dve", lambda e: e.tensor_tensor(out=col(drv, DV_TMP, 16), in0=col(drv, DV_TMP, 16),
                                               in1=col(drv, DV_E, 16), op=ALU.mult), reads=kdrv, writes=kdrv)
        P.add("dve", lambda e: e.tensor_scalar(out=col(drv, DV_TMP, 16), in0=col(drv, DV_TMP, 16), scalar1=-1.0,
                                               scalar2=1.0, op0=ALU.mult, op1=ALU.add), reads=kdrv, writes=kdrv)
        P.add("dve", lambda e: e.tensor_tensor(out=col(drv, DV_SP, 16), in0=col(drv, DV_TMP, 16),
                                               in1=col(drv, DV_E, 16), op=ALU.mult), reads=kdrv, writes=kdrv)
        P.add("dve", lambda e: e.tensor_scalar(out=col(drv, DV_HCL, 16), in0=col(drv, DV_SP, 16), scalar1=-4.0,
                                               scalar2=None, op0=ALU.mult), reads=kdrv, writes=kdrv)
        P.add("dve", lambda e: e.tensor_scalar(out=col(drv, DV_HBA, 16), in0=col(cv, CV_BA, 16), scalar1=0.5,
                                               scalar2=None, op0=ALU.mult), reads=kcv, writes=kdrv)
        P.add("dve", lambda e: e.tensor_scalar(out=col(drv, DV_HBI, 16), in0=col(cv, CV_BI, 16), scalar1=0.5,
                                               scalar2=None, op0=ALU.mult), reads=kcv, writes=kdrv)
        P.add("dve", lambda e: e.memset(sm[:, :], 0.0), writes=sm.k() + SM_KEYS)

        ring_ctr = [0]

        def load_unit(parts):
            s = ring_ctr[0] % NSLOT
            ring_ctr[0] += 1
            sl = slots[s]
            ops = []
            for i, (eo, src, kc_, n_) in enumerate(parts):
                def f(e, eo=eo, src=src, kc_=kc_, n_=n_, sl=sl):
                    dst = sl[:, eo:eo + kc_ * n_].rearrange("p (k n) -> p k n", n=n_)
                    return e.dma_start(out=dst, in_=src)
                ops.append(P.add("pool", f, writes=sl.k((eo, eo + kc_ * n_)), kind="dma", dsem="slot%d" % s))
            for o in ops:
                o.val = ops[-1].val
            return s

        def wsrc(w, r0, nk, c0, n):
            return w[r0:r0 + nk * 128, c0:c0 + n].rearrange("(k p) n -> p k n", p=128)

        def wv(s, eo, kc_, n_):
            return slots[s][:, eo:eo + kc_ * n_].rearrange("p (k n) -> p k n", n=n_)

        def mm_group(out_ap, pskeys, lhs_list, rhs_list, reads):
            def f(e):
                n = len(lhs_list)
                ins = None
                for i in range(n):
                    ins = e.matmul(out_ap, lhsT=lhs_list[i], rhs=rhs_list[i], start=(i == 0), stop=(i == n - 1))
                return ins
            return P.add("pe", f, reads=reads, writes=pskeys)

        rot = {"ss": 0, "x": 0, "xs": 0, "o": 0, "xn": 0}

        def norm_A(src_ap, src_keys, npart):
            sc = rot["ss"] % 16
            rot["ss"] += 1
            xn = xns[rot["xn"] % 4]
            rot["xn"] += 1
            kss, ksd, krs = (("sm", SM_SS + sc),), (("sm", SM_SD + sc),), (("sm", SM_RS + sc),)
            ss = sm[0:npart, SM_SS + sc:SM_SS + sc + 1]
            sd = sm[0:npart, SM_SD + sc:SM_SD + sc + 1]
            rs = sm[0:npart, SM_RS + sc:SM_RS + sc + 1]
            P.add("act", lambda e: e.activation(out=xn[0:npart, :], in_=src_ap, func=AF.Square, accum_out=ss),
                  reads=src_keys, writes=xn.k() + kss)
            P.add("act", lambda e: e.activation(out=sd, in_=ss, func=AF.Sqrt, scale=1.0 / D, bias=EPS),
                  reads=kss, writes=ksd)
            P.add("dve", lambda e: e.reciprocal(out=rs, in_=sd), reads=ksd, writes=krs)
            P.add("act", lambda e: e.activation(out=xn[0:npart, :], in_=src_ap, func=AF.Copy, scale=rs),
                  reads=src_keys + krs, writes=xn.k())
            return (xn, npart)

        def norm_B(ctx, gcol, dst_view, dst_col0, ncols):
            xn, npart = ctx
            for g4 in range(4):
                b = nbank()
                pv = ps[b][:, :].bitcast(BF16).rearrange("p (a t) -> p a t", t=128)[:, 0:4, :]

                def ft(e, g4=g4, pv=pv, xn=xn):
                    ins = None
                    for j in range(4):
                        c = g4 * 4 + j
                        ins = e.transpose(out=pv[:, j, 0:npart], in_=xn[0:npart, c * 128:(c + 1) * 128],
                                          identity=ident_bf[0:npart, 0:npart])
                    return ins
                P.add("pe", ft, reads=xn.k((g4 * 512, g4 * 512 + 512)) + ident_bf.k(), writes=PK(b))

                def fe(e, g4=g4, pv=pv):
                    gsl = cv[:, gcol + g4 * 4:gcol + g4 * 4 + 4]
                    return e.tensor_tensor(out=dst_view[:, g4 * 4:g4 * 4 + 4, dst_col0:dst_col0 + ncols],
                                           in0=pv[:, :, 0:npart],
                                           in1=gsl.unsqueeze(2).to_broadcast([128, 4, npart]), op=ALU.mult)
                P.add("dve", fe, reads=PK(b) + kcv,
                      writes=dst_view.k((g4 * 4, g4 * 4 + 4), (dst_col0, dst_col0 + ncols)))

        def norm_to_fm(src_ap, src_keys, npart, gcol, dst_view, dst_col0, ncols):
            norm_B(norm_A(src_ap, src_keys, npart), gcol, dst_view, dst_col0, ncols)

        cc_u = [0]

        pending_rb = []

        def exchange_readback():
            u, dst_view = pending_rb.pop()
            P.add("sp", lambda e: e.dma_start(out=dst_view[:, :, :], in_=ccout[u].rearrange("(c p) n -> p c n", p=128)),
                  reads=(("ccout", u),), writes=dst_view.k(), kind="dma", dsem="ccb")

        def exchange(src_ap, src_keys, dst_view, in_queue="sp", defer_readback=False):
            u = cc_u[0]
            cc_u[0] += 1
            kin, kout = (("ccin", u),), (("ccout", u),)
            P.add(in_queue, lambda e: e.dma_start(out=ccin[u], in_=src_ap), reads=src_keys, writes=kin,
                  kind="dma", dsem="cci" if in_queue == "sp" else "ccip")
            P.add("pool", lambda e: e.collective_compute("AllGather", ALU.bypass,
                                                         replica_groups=[[0, 1, 2, 3], [4, 5, 6, 7]],
                                                         ins=[ccin[u]], outs=[ccout[u]]),
                  reads=kin, writes=kout, kind="cc", dsem="cc", inc=1)
            pending_rb.append((u, dst_view))
            if not defer_readback:
                exchange_readback()

        st0_done = set()

        def st0_steps(r):
            state = {"prevB": None}
            steps = []
            for ti in range(-1, NT):
                def step(ti=ti):
                    par = rot["x"] % 2
                    rot["x"] += 1
                    if ti < 0:
                        npart, src = HALO, xh[r, 0:HALO, :]
                    else:
                        npart, src = 128, xh[r, HALO + ti * 128:HALO + (ti + 1) * 128, :]
                    P.add("sp", lambda e, par=par, npart=npart, src=src: e.dma_start(out=xst[0:npart, par, :], in_=src),
                          writes=xst.k(par), kind="dma", dsem="x%d" % par)
                    ctx = norm_A(xst[0:npart, par, :], xst.k(par), npart)
                    if state["prevB"] is not None:
                        norm_B(*state["prevB"])
                    state["prevB"] = (ctx, CV_GMIX, hhalo, 0, HALO) if ti < 0 else (ctx, CV_GMIX, hB, ti * 128, 128)
                steps.append(step)
            steps.append(lambda: norm_B(*state["prevB"]))
            return steps

        cur_round = [0]

        def final_A(tt):
            sc = rot["ss"] % 16
            rot["ss"] += 1
            kss, ksd = (("sm", SM_SS + sc),), (("sm", SM_SD + sc),)
            ss, sd = col(sm, SM_SS + sc), col(sm, SM_SD + sc)
            oi = rot["o"] % 2
            rot["o"] += 1
            ob = ost[oi]
            P.add("act", lambda e, tt=tt, ss=ss, ob=ob: e.activation(out=ob[:, :], in_=xres[:, tt, :], func=AF.Square,
                                                                     accum_out=ss), reads=xres.k(tt), writes=ob.k() + kss)
            P.add("act", lambda e, ss=ss, sd=sd: e.activation(out=sd, in_=ss, func=AF.Sqrt, scale=1.0 / D, bias=EPS),
                  reads=kss, writes=ksd)
            return (tt, sc, oi)

        def final_B(ctx):
            tt, sc, oi = ctx
            r = cur_round[0]
            ksd, krs = (("sm", SM_SD + sc),), (("sm", SM_RS + sc),)
            sd, rs = col(sm, SM_SD + sc), col(sm, SM_RS + sc)
            ob = ost[oi]
            P.add("dve", lambda e, sd=sd, rs=rs: e.reciprocal(out=rs, in_=sd), reads=ksd, writes=krs)
            P.add("dve", lambda e, tt=tt, rs=rs, ob=ob: e.scalar_tensor_tensor(
                out=ob[:, :], in0=xres[:, tt, :], scalar=rs, in1=gfin[:, :], op0=ALU.mult, op1=ALU.mult),
                reads=xres.k(tt) + krs + gfin.k(), writes=ob.k())
            P.add("sp", lambda e, tt=tt, ob=ob, r=r: e.dma_start(out=outd[r, tt * 128:(tt + 1) * 128, :], in_=ob[:, :]),
                  reads=ob.k(), writes=(("out", r, tt),), kind="dma", dsem="o%d" % oi)

        for r in range(R):
            cur_round[0] = r
            if r not in st0_done:
                for st in st0_steps(r):
                    st()

            zt = tf[7]
            P.add("dve", lambda e: e.memset(zt[:, 0:T], 0.0), writes=zt.k((0, T)))

            ul, Hb, Ab, gl0 = tf[0], tf[4], tf[5], tf[6]
            gls = [gl0, gl1]

            def lru_load(hb):
                sA = load_unit([(0, wsrc(w_in, 0, 16, 1024 + hb * 256, 256), 16, 256)])
                sB = load_unit([(0, wsrc(w_in, 0, 16, 3072 + hb * 256, 256), 16, 256),
                                (4096, wsrc(w_a, hb * 256, 2, 0, 256), 2, 256),
                                (4608, wsrc(w_i, hb * 256, 2, 0, 256), 2, 256)])
                if hb <= 5:
                    bsets = [(tf[1], tf[2], tf[3]), (ex[0], ex[1], ex[2])]
                elif hb == 6:
                    bsets = [(tf[1], tf[2], tf[3]), (ex[1], ex[3], tf[3])]
                else:
                    bsets = [(tf[1], tf[2], tf[3]), (tf[1], tf[2], tf[3])]
                return dict(hb=hb, wl=wv(sA, 0, 16, 256), wa=wv(sB, 4096, 2, 256), wi=wv(sB, 4608, 2, 256),
                            wg=wv(sB, 0, 16, 256), kA_l=slots[sA].k((0, 4096)), kA_g=slots[sB].k((4096, 5120)),
                            kB=slots[sB].k((0, 4096)), bsets=bsets,
                            gls=[gl0, ex[3] if (hb <= 5 and hb % 2 == 1) else gl1])

            def lru_X(cx, ci):
                hb, wl, kA_l = cx["hb"], cx["wl"], cx["kA_l"]
                c = 2 * hb + ci
                segs = [(None, HALO, 0)] + [(tb, TB, HALO + tb * TB) for tb in range(2)]
                for (tb, n, uoff) in segs:
                    b = nbank()
                    if tb is None:
                        rhs = [hhalo[:, k, :] for k in range(KC)]
                        rk = hhalo.k()
                    else:
                        rhs = [hB[:, k, tb * TB:(tb + 1) * TB] for k in range(KC)]
                        rk = hB.k(None, (tb * TB, (tb + 1) * TB))
                    mm_group(ps[b][:, 0:n], PK(b), [wl[:, k, ci * 128:(ci + 1) * 128] for k in range(KC)], rhs,
                             kA_l + rk)
                    P.add("act", lambda e, b=b, n=n, uoff=uoff: e.activation(out=ul[:, uoff:uoff + n],
                                                                             in_=ps[b][:, 0:n], func=AF.Copy),
                          reads=PK(b), writes=ul.k((uoff, uoff + n)))
                P.add("act", lambda e, c=c, ci=ci: e.activation(out=vv[:, ci, :], in_=ul[:, HALO:HALO + T],
                                                                func=AF.Identity,
                                                                scale=col(cv, CV_LCW + c * 4 + 3),
                                                                bias=col(cv, CV_LCB + c)),
                      reads=ul.k() + kcv, writes=vv.k(ci))
                for kk in range(3):
                    P.add("dve", lambda e, c=c, ci=ci, kk=kk: e.scalar_tensor_tensor(
                        out=vv[:, ci, :], in0=ul[:, HALO - 3 + kk:HALO - 3 + kk + T],
                        scalar=col(cv, CV_LCW + c * 4 + kk), in1=vv[:, ci, :], op0=ALU.mult, op1=ALU.add),
                        reads=ul.k() + kcv + vv.k(ci), writes=vv.k(ci))
                P.add("act", lambda e, ci=ci: e.activation(out=vb[:, ci, :], in_=vv[:, ci, :], func=AF.Copy),
                      reads=vv.k(ci), writes=vb.k(ci))

            def lru_G(cx):
                wg, kB = cx["wg"], cx["kB"]
                for ci in range(2):
                    gl = cx["gls"][ci]
                    for tb in range(2):
                        b = nbank()
                        mm_group(ps[b][:, :], PK(b), [wg[:, k, ci * 128:(ci + 1) * 128] for k in range(KC)],
                                 [hB[:, k, tb * TB:(tb + 1) * TB] for k in range(KC)],
                                 kB + hB.k(None, (tb * TB, (tb + 1) * TB)))
                        P.add("act", lambda e, b=b, tb=tb, gl=gl: e.activation(out=gl[:, tb * TB:(tb + 1) * TB],
                                                                               in_=ps[b][:, :], func=AF.Gelu_apprx_tanh),
                              reads=PK(b), writes=gl.k((tb * TB, (tb + 1) * TB)))

            def gates(cx, ci):
                hb, wa, wi, kA_g = cx["hb"], cx["wa"], cx["wi"], cx["kA_g"]
                c = 2 * hb + ci
                thr, thi, mt = cx["bsets"][ci]
                for (wmat, bcol, dstv) in ((wa, DV_HBA, thr), (wi, DV_HBI, thi)):
                    for tb in range(2):
                        b = nbank()
                        mm_group(ps[b][:, :], PK(b), [wmat[:, k, ci * 128:(ci + 1) * 128] for k in range(2)],
                                 [vb[:, k, tb * TB:(tb + 1) * TB] for k in range(2)],
                                 kA_g + vb.k(None, (tb * TB, (tb + 1) * TB)))
                        P.add("act", lambda e, b=b, tb=tb, dstv=dstv, bcol=bcol, c=c: e.activation(
                            out=dstv[:, tb * TB:(tb + 1) * TB], in_=ps[b][:, :], func=AF.Tanh, scale=0.5,
                            bias=col(drv, bcol + c)),
                            reads=PK(b) + kdrv, writes=dstv.k((tb * TB, (tb + 1) * TB)))

            def act_exp(cx, ci):
                c = 2 * cx["hb"] + ci
                thr, thi, mt = cx["bsets"][ci]
                kthr = thr.k((0, T))
                P.add("act", lambda e, c=c, thr=thr: e.activation(out=thr[:, 0:T], in_=thr[:, 0:T], func=AF.Exp,
                                                                  scale=col(drv, DV_HCL + c),
                                                                  bias=col(drv, DV_HCL + c)),
                      reads=kthr + kdrv, writes=kthr)

            def act_sq(cx, ci):
                thr, thi, mt = cx["bsets"][ci]
                P.add("act", lambda e, thr=thr, mt=mt: e.activation(out=mt[:, 0:T], in_=thr[:, 0:T], func=AF.Square),
                      reads=thr.k((0, T)), writes=mt.k((0, T)))

            def act_sqrt(cx, ci):
                thr, thi, mt = cx["bsets"][ci]
                P.add("act", lambda e, mt=mt: e.activation(out=mt[:, 0:T], in_=mt[:, 0:T], func=AF.Sqrt, scale=-0.25,
                                                           bias=0.25), reads=mt.k((0, T)), writes=mt.k((0, T)))

            def dve_chain(cx, ci, part):
                c = 2 * cx["hb"] + ci
                thr, thi, mt = cx["bsets"][ci]
                gl = cx["gls"][ci]
                kthr, kthi, kmt = thr.k((0, T)), thi.k((0, T)), mt.k((0, T))
                kH, kAb = Hb.k((0, T)), Ab.k((0, T))
                if part == "tail":
                    kgl = gl.k((0, T))
                    P.add("dve", lambda e, c=c, gl=gl: e.tensor_tensor(out=ylru[:, c, :], in0=Hb[:, 0:T], in1=gl[:, 0:T],
                                                                       op=ALU.mult), reads=kH + kgl, writes=ylru.k(c))
                    P.add("dve", lambda e, c=c, gl=gl: e.tensor_tensor(out=p2[:, c, :], in0=Ab[:, 0:T], in1=gl[:, 0:T],
                                                                       op=ALU.mult), reads=kAb + kgl, writes=p2.k(c))
                    kc1 = (("sm", "cc1"),)
                    P.add("dve", lambda e, c=c: e.tensor_copy(out=col(sm, SM_CC1 + 2 * c), in_=Hb[:, T - 1:T]),
                          reads=kH, writes=kc1)
                    P.add("dve", lambda e, c=c: e.tensor_copy(out=col(sm, SM_CC1 + 2 * c + 1), in_=Ab[:, T - 1:T]),
                          reads=kAb, writes=kc1)
                    return
                if part in ("head", "t"):
                    P.add("dve", lambda e, ci=ci, thi=thi: e.scalar_tensor_tensor(out=thi[:, 0:T], in0=thi[:, 0:T],
                                                                                  scalar=1.0, in1=vv[:, ci, :],
                                                                                  op0=ALU.add, op1=ALU.mult),
                          reads=kthi + vv.k(ci), writes=kthi)
                if part == "t":
                    return
                P.add("dve", lambda e, thi=thi, mt=mt: e.tensor_tensor(out=thi[:, 0:T], in0=thi[:, 0:T], in1=mt[:, 0:T],
                                                                       op=ALU.mult), reads=kthi + kmt, writes=kthi)
                P.add("dve", lambda e, thr=thr, thi=thi: e.tensor_tensor_scan(
                    out=Hb[:, 0:T], data0=thr[:, 0:T], data1=thi[:, 0:T], initial=0.0, op0=ALU.mult, op1=ALU.add),
                    reads=kthr + kthi, writes=kH)
                P.add("dve", lambda e, thr=thr: e.tensor_tensor_scan(
                    out=Ab[:, 0:T], data0=thr[:, 0:T], data1=zt[:, 0:T], initial=1.0, op0=ALU.mult, op1=ALU.add),
                    reads=kthr + zt.k((0, T)), writes=kAb)

            cx = lru_load(0)
            lru_X(cx, 0)
            lru_X(cx, 1)
            lru_G(cx)
            for hb in range(8):
                bsets = cx["bsets"]
                dbl_t = bsets[0][0] is not bsets[1][0]
                dbl_m = bsets[0][2] is not bsets[1][2]
                nxt_box = [None]

                def chain_and_next(ci):
                    if ci == 1 and hb + 1 < 8 and nxt_box[0] is not None:
                        dve_chain(cx, ci, "t")
                        lru_X(nxt_box[0], ci)
                        dve_chain(cx, ci, "mid")
                        dve_chain(cx, ci, "tail")
                    else:
                        dve_chain(cx, ci, "head")
                        tail_and_next(ci)

                def tail_and_next(ci):
                    if hb + 1 < 8:
                        if nxt_box[0] is None:
                            nxt_box[0] = lru_load(hb + 1)
                        lru_X(nxt_box[0], ci)
                    dve_chain(cx, ci, "tail")

                if dbl_t:
                    gates(cx, 0)
                    gates(cx, 1)
                    act_exp(cx, 0)
                    act_exp(cx, 1)
                    if dbl_m:
                        act_sq(cx, 0)
                        act_sq(cx, 1)
                        act_sqrt(cx, 0)
                        act_sqrt(cx, 1)
                        chain_and_next(0)
                        chain_and_next(1)
                    else:
                        act_sq(cx, 0)
                        act_sqrt(cx, 0)
                        chain_and_next(0)
                        act_sq(cx, 1)
                        act_sqrt(cx, 1)
                        chain_and_next(1)
                else:
                    for ci in range(2):
                        gates(cx, ci)
                        act_exp(cx, ci)
                        act_sq(cx, ci)
                        act_sqrt(cx, ci)
                        chain_and_next(ci)
                if nxt_box[0] is not None:
                    lru_G(nxt_box[0])
                    cx = nxt_box[0]

            merged = p2

            def merge_ctx(j):
                sM = load_unit([(0, wsrc(w_pp, 0, 8, j * 128, 128), 8, 128),
                                (1024, wsrc(w_lp, 0, 16, j * 128, 128), 16, 128),
                                (3072, wsrc(w_in, 0, 16, 5120 + j * 128, 128), 16, 128),
                                (5120, wsrc(w_in, 0, 16, 7168 + j * 128, 128), 16, 128)])
                if j == 0:
                    bufs = [(tf[3], tf[4], tf[7], tf[0]), (tf[5], tf[6], tf[1], tf[2])]
                else:
                    bufs = [(tf[0], tf[1], tf[2], tf[3]), (tf[4], tf[5], tf[6], tf[7])]
                return dict(j=j, wpp=wv(sM, 0, 8, 128), wlp=wv(sM, 1024, 16, 128), wg0=wv(sM, 3072, 16, 128),
                            wg1=wv(sM, 5120, 16, 128), kM=slots[sM].k((0, 7168)), bufs=bufs, banks={})

            def merge_g(mc, tb):
                j, kM = mc["j"], mc["kM"]
                tsl = slice(tb * TB, (tb + 1) * TB)
                tr = (tb * TB, (tb + 1) * TB)
                g0t, g1t, m0t, m1t = mc["bufs"][tb]
                for (wg, gt, boff) in ((mc["wg0"], g0t, 0), (mc["wg1"], g1t, 16)):
                    b = nbank()
                    mm_group(ps[b][:, :], PK(b), [wg[:, k, :] for k in range(16)], [hB[:, k, tsl] for k in range(16)],
                             kM + hB.k(None, tr))
                    P.add("act", lambda e, b=b, gt=gt, j=j, boff=boff: e.activation(
                        out=gt[:, 0:TB], in_=ps[b][:, :], func=AF.Sigmoid, bias=col(cv, CV_BGATE + boff + j)),
                        reads=PK(b) + kcv, writes=gt.k((0, TB)))

            def merge_P(mc, tb):
                kM = mc["kM"]
                tsl = slice(tb * TB, (tb + 1) * TB)
                tr = (tb * TB, (tb + 1) * TB)
                g0t, g1t, m0t, m1t = mc["bufs"][tb]
                bP = nbank()
                mm_group(ps[bP][:, :], PK(bP), [mc["wpp"][:, k, :] for k in range(8)], [ypool[:, k, tsl] for k in range(8)],
                         kM + ypool.k(None, tr))
                P.add("dve", lambda e, bP=bP, g0t=g0t, m0t=m0t: e.tensor_tensor(out=m0t[:, 0:TB], in0=g0t[:, 0:TB],
                                                                                in1=ps[bP][:, :], op=ALU.mult),
                      reads=PK(bP) + g0t.k((0, TB)), writes=m0t.k((0, TB)))

            def merge_p2(mc, tb):
                j, kM = mc["j"], mc["kM"]
                tsl = slice(tb * TB, (tb + 1) * TB)
                tr = (tb * TB, (tb + 1) * TB)
                g0t, g1t, m0t, m1t = mc["bufs"][tb]
                bL = nbank()
                mm_group(ps[bL][:, :], PK(bL), [mc["wlp"][:, k, :] for k in range(16)], [ylru[:, k, tsl] for k in range(16)],
                         kM + ylru.k(None, tr))
                P.add("dve", lambda e, bL=bL, g1t=g1t, m1t=m1t: e.tensor_tensor(out=m1t[:, 0:TB], in0=g1t[:, 0:TB],
                                                                                in1=ps[bL][:, :], op=ALU.mult),
                      reads=PK(bL) + g1t.k((0, TB)), writes=m1t.k((0, TB)))
                P.add("dve", lambda e, m0t=m0t, m1t=m1t, j=j, tsl=tsl: e.tensor_tensor(
                    out=merged[:, j, tsl], in0=m0t[:, 0:TB], in1=m1t[:, 0:TB], op=ALU.add),
                    reads=m0t.k((0, TB)) + m1t.k((0, TB)), writes=merged.k(j, tr))

            up, s1, s2 = tf[0], tf[1], tf[2]

            def pool_front(g):
                dbuf = dbufs[g % 2]
                sP = load_unit([(0, wsrc(w_in, 0, 16, g * 256, 256), 16, 256)])
                wu = wv(sP, 0, 16, 256)
                kP_u = slots[sP].k((0, 4096))
                for ci in range(2):
                    segs = [(None, HALO, 0)] + [(tb, TB, HALO + tb * TB) for tb in range(2)]
                    for (tb, n, uoff) in segs:
                        b = nbank()
                        if tb is None:
                            rhs = [hhalo[:, k, :] for k in range(KC)]
                            rk = hhalo.k()
                        else:
                            rhs = [hB[:, k, tb * TB:(tb + 1) * TB] for k in range(KC)]
                            rk = hB.k(None, (tb * TB, (tb + 1) * TB))
                        mm_group(ps[b][:, 0:n], PK(b), [wu[:, k, ci * 128:(ci + 1) * 128] for k in range(KC)], rhs,
                                 kP_u + rk)
                        P.add("act", lambda e, b=b, n=n, uoff=uoff: e.activation(out=up[:, uoff:uoff + n],
                                                                                 in_=ps[b][:, 0:n], func=AF.Copy),
                              reads=PK(b), writes=up.k((uoff, uoff + n)))
                    L = HALO + T
                    srcv, lo = up, 0
                    bufs = [s1, s2]
                    for lvl in range(g + 1):
                        sh = 1 << lvl
                        dstv = bufs[lvl % 2]
                        nlo = lo + sh
                        P.add("dve", lambda e, srcv=srcv, dstv=dstv, nlo=nlo, sh=sh, L=L: e.tensor_tensor(
                            out=dstv[:, nlo:L], in0=srcv[:, nlo:L], in1=srcv[:, nlo - sh:L - sh], op=ALU.add),
                            reads=srcv.k((lo, L)), writes=dstv.k((nlo, L)))
                        srcv, lo = dstv, nlo
                    Sv = srcv
                    w = POOL_W[g]
                    P.add("dve", lambda e, Sv=Sv, ci=ci, w=w, L=L: e.scalar_tensor_tensor(
                        out=dbuf[:, ci, :], in0=Sv[:, HALO:L], scalar=1.0 / w, in1=up[:, HALO:L], op0=ALU.mult,
                        op1=ALU.subtract), reads=Sv.k((HALO, L)) + up.k((HALO, L)), writes=dbuf.k(ci))
                    ktmp = (("sm", "t16"),)
                    P.add("dve", lambda e, Sv=Sv, g=g, r=r: e.tensor_tensor(
                        out=sm[:, SM_T16:SM_T16 + 16], in0=Sv[:, HALO:HALO + 16],
                        in1=cv[:, CV_INVC + (r * 4 + g) * 16:CV_INVC + (r * 4 + g) * 16 + 16], op=ALU.mult),
                        reads=Sv.k((HALO, HALO + 16)) + kcv, writes=ktmp)
                    P.add("dve", lambda e, ci=ci: e.tensor_tensor(out=dbuf[:, ci, 0:16], in0=sm[:, SM_T16:SM_T16 + 16],
                                                                  in1=up[:, HALO:HALO + 16], op=ALU.subtract),
                          reads=ktmp + up.k((HALO, HALO + 16)) + dbuf.k(ci), writes=dbuf.k(ci))
                return sP

            def pool_back(g, sP):
                dbuf = dbufs[g % 2]
                kP_p = wpool_sb.k((2 * g, 2 * g + 2))
                for m in range(2):
                    for tb in range(2):
                        b = nbank()
                        mm_group(ps[b][:, :], PK(b), [wpool_sb[:, 2 * g + k, m * 128:(m + 1) * 128] for k in range(2)],
                                 [dbuf[:, k, tb * TB:(tb + 1) * TB] for k in range(2)],
                                 kP_p + dbuf.k(None, (tb * TB, (tb + 1) * TB)))
                        P.add("act", lambda e, b=b, tb=tb, cc=2 * g + m: e.activation(
                            out=ypool[:, cc, tb * TB:(tb + 1) * TB], in_=ps[b][:, :], func=AF.Identity,
                            scale=col(cv, CV_PSCALE + cc)),
                            reads=PK(b) + kcv, writes=ypool.k(2 * g + m, (tb * TB, (tb + 1) * TB)))


            khin, ksc, kccg = (("sm", "hin"),), (("sm", "sc"),), ccg.k()
            hin = sm[:, SM_HIN:SM_HIN + 16]
            scar = sm[:, SM_SC:SM_SC + 16]

            def fold():
                P.add("dve", lambda e: e.memset(hin, 0.0), writes=khin)
                for j in range(4):
                    P.add("dve", lambda e, j=j: e.scalar_tensor_tensor(out=hin, in0=scar, scalar=col(cv, CV_SEL + j),
                                                                       in1=hin, op0=ALU.mult, op1=ALU.add),
                          reads=khin + ksc + kcv, writes=khin)
                    gj = ccg[:, j, :].rearrange("p (c two) -> p c two", two=2)
                    P.add("dve", lambda e, gj=gj: e.tensor_tensor(out=scar, in0=scar, in1=gj[:, :, 1], op=ALU.mult),
                          reads=ksc + kccg, writes=ksc)
                    P.add("dve", lambda e, gj=gj: e.tensor_tensor(out=scar, in0=scar, in1=gj[:, :, 0], op=ALU.add),
                          reads=ksc + kccg, writes=ksc)

            sPs = {}
            for g in range(4):
                if g == 2:
                    exchange(sm[:, SM_CC1:SM_CC1 + 32], (("sm", "cc1"),), ccg)
                sPs[g] = pool_front(g)
                if g == 1:
                    P.add("pool", lambda e: e.dma_start(out=wpool_sb[:, :, :],
                                                        in_=w_pool.rearrange("(k p) n -> p k n", p=128)),
                          writes=wpool_sb.k(), kind="dma", dsem="wp")
                if g >= 1:
                    pool_back(g - 1, sPs[g - 1])
            mc0 = merge_ctx(0)
            merge_g(mc0, 0)
            merge_g(mc0, 1)
            pool_back(3, sPs[3])
            fold()

            for tb in range(2):
                for c in range(16):
                    P.add("dve", lambda e, c=c, tb=tb: e.scalar_tensor_tensor(
                        out=ylru[:, c, tb * TB:(tb + 1) * TB], in0=p2[:, c, tb * TB:(tb + 1) * TB],
                        scalar=col(sm, SM_HIN + c), in1=ylru[:, c, tb * TB:(tb + 1) * TB], op0=ALU.mult, op1=ALU.add),
                        reads=p2.k(c, (tb * TB, (tb + 1) * TB)) + ylru.k(c, (tb * TB, (tb + 1) * TB)) + khin,
                        writes=ylru.k(c, (tb * TB, (tb + 1) * TB)))

            for j in range(16):
                if j == 0:
                    mc = mc0
                else:
                    mc = merge_ctx(j)
                if j == 0:
                    merge_P(mc, 0)
                    merge_P(mc, 1)
                    merge_p2(mc, 0)
                    merge_p2(mc, 1)
                else:
                    for tb in range(2):
                        merge_g(mc, tb)
                        merge_P(mc, tb)
                        merge_p2(mc, tb)

            tt_order = [7, 0, 1, 2, 3, 4, 5, 6]
            kc2 = (("sm", "cc2"),)
            cc2 = sm[:, SM_CC2:SM_CC2 + 32]
            pre_ffn = {}
            for cb in range(4):
                sO = load_unit([(0, wsrc(w_out, 0, 16, cb * 512, 512), 16, 512)])
                wo = wv(sO, 0, 16, 512)
                kO = slots[sO].k()
                if cb == 3:
                    pre_ffn[(0, 0)] = load_unit([(0, wsrc(w_up, 0, 16, 0, 256), 16, 256),
                                                 (4096, wsrc(w_up, 0, 16, DFF, 256), 16, 256)])
                ctxs = {}
                for i, tt in enumerate(tt_order):
                    if cb == 3 and i >= 1:
                        tA = tt_order[i - 1]
                        ctxs[tA] = norm_A(xres[:, tA, :], xres.k(tA), 128)
                    xi = rot["xs"] % 4
                    rot["xs"] += 1
                    xsb = tf[xi]
                    kxs = xsb.k((0, TB))
                    P.add("sp", lambda e, xsb=xsb, tt=tt, cb=cb, r=r: e.dma_start(
                        out=xsb[:, 0:TB], in_=xh[r, HALO + tt * 128:HALO + (tt + 1) * 128, cb * 512:(cb + 1) * 512]),
                        writes=kxs, kind="dma", dsem="xs%d" % xi)
                    b = nbank()
                    tr = (tt * 128, (tt + 1) * 128)
                    mm_group(ps[b][:, :], PK(b), [merged[:, k, tt * 128:(tt + 1) * 128] for k in range(KC)],
                             [wo[:, k, :] for k in range(KC)], kO + merged.k(None, tr))
                    P.add("dve", lambda e, b=b, xsb=xsb, tt=tt, cb=cb: e.tensor_tensor(
                        out=xres[:, tt, cb * 512:(cb + 1) * 512], in0=ps[b][:, :], in1=xsb[:, 0:TB], op=ALU.add),
                        reads=PK(b) + kxs, writes=xres.k(tt, (cb * 512, (cb + 1) * 512)))
                    if cb == 3 and i >= 2:
                        t2 = tt_order[i - 2]
                        norm_B(ctxs[t2], CV_GMLP, hB, t2 * 128, 128)
                        if t2 == 7:
                            P.add("dve", lambda e: e.tensor_copy(out=cc2.rearrange("p (c two) -> p c two", two=2),
                                                                 in_=hB[:, :, T - 2:T]),
                                  reads=hB.k(None, (T - 2, T)), writes=kc2)
                            exchange(cc2, kc2, ccg2, in_queue="pool", defer_readback=True)
            t6, t5 = tt_order[-1], tt_order[-2]
            norm_B(ctxs[t5], CV_GMLP, hB, t5 * 128, 128)
            norm_to_fm(xres[:, t6, :], xres.k(t6), 128, CV_GMLP, hB, t6 * 128, 128)
            exchange_readback()

            khh, kgc, kccg2 = (("sm", "hh"),), (("sm", "gc"),), ccg2.k()
            hh = sm[:, SM_HH:SM_HH + 32]
            gcar = sm[:, SM_GC:SM_GC + 32]
            P.add("dve", lambda e: e.tensor_scalar(out=hh, in0=gcar, scalar1=col(cv, CV_SEL + 0), scalar2=None,
                                                   op0=ALU.mult), reads=kgc + kcv, writes=khh)
            for j in range(1, 4):
                P.add("dve", lambda e, j=j: e.scalar_tensor_tensor(out=hh, in0=ccg2[:, j - 1, :],
                                                                   scalar=col(cv, CV_SEL + j), in1=hh,
                                                                   op0=ALU.mult, op1=ALU.add),
                      reads=khh + kccg2 + kcv, writes=khh)
            P.add("dve", lambda e: e.tensor_copy(out=h2halo[:, :, :], in_=hh.rearrange("p (c two) -> p c two", two=2)),
                  reads=khh, writes=h2halo.k())
            P.add("dve", lambda e: e.tensor_copy(out=gcar, in_=ccg2[:, 3, :]), reads=kccg2, writes=kgc)

            gv = p2
            def pair_ctx(grp, pr, alt=False):
                f0 = grp * 16 + 2 * pr
                if (grp, pr) in pre_ffn:
                    sU = pre_ffn[(grp, pr)]
                else:
                    sU = load_unit([(0, wsrc(w_up, 0, 16, f0 * 128, 256), 16, 256),
                                    (4096, wsrc(w_up, 0, 16, DFF + f0 * 128, 256), 16, 256)])
                wgt, wvl = wv(sU, 0, 16, 256), wv(sU, 4096, 16, 256)
                kU = slots[sU].k()
                res = []
                for fi in range(2):
                    f = f0 + fi
                    o = 2 * (f % 2)
                    base = 0 if alt else 4
                    res.append(dict(f=f, fl=f - grp * 16, gp=tf[base + o], cvb=tf[base + 1 + o], kU=kU,
                                    lw=[wgt[:, k, fi * 128:(fi + 1) * 128] for k in range(KC)],
                                    lv=[wvl[:, k, fi * 128:(fi + 1) * 128] for k in range(KC)]))
                return res

            def ffn_gate(c):
                gp, lw, kU = c["gp"], c["lw"], c["kU"]
                for tb in range(2):
                    b = nbank()
                    mm_group(ps[b][:, :], PK(b), lw, [hB[:, k, tb * TB:(tb + 1) * TB] for k in range(KC)],
                             kU + hB.k(None, (tb * TB, (tb + 1) * TB)))
                    P.add("act", lambda e, b=b, gp=gp, tb=tb: e.activation(
                        out=gp[:, 2 + tb * TB:2 + (tb + 1) * TB], in_=ps[b][:, :], func=AF.Copy),
                        reads=PK(b), writes=gp.k((2 + tb * TB, 2 + (tb + 1) * TB)))

            def ffn_halo(c):
                gp, lw, kU = c["gp"], c["lw"], c["kU"]
                b = nbank()
                mm_group(ps[b][:, 0:2], PK(b), lw, [h2halo[:, k, :] for k in range(KC)], kU + h2halo.k())
                P.add("act", lambda e, b=b, gp=gp: e.activation(out=gp[:, 0:2], in_=ps[b][:, 0:2], func=AF.Copy),
                      reads=PK(b), writes=gp.k((0, 2)))

            def ffn_post(c):
                gp, cvb, f = c["gp"], c["cvb"], c["f"]
                kgp, kcb = gp.k((0, T + 2)), cvb.k((0, T))
                P.add("act", lambda e, gp=gp, cvb=cvb, f=f: e.activation(
                    out=cvb[:, 0:T], in_=gp[:, 2:2 + T], func=AF.Identity, scale=col(cv, CV_FCW + f * 3 + 2),
                    bias=col(cv, CV_FCB + f)), reads=kgp + kcv, writes=kcb)
                for kk in range(2):
                    P.add("dve", lambda e, gp=gp, cvb=cvb, f=f, kk=kk: e.scalar_tensor_tensor(
                        out=cvb[:, 0:T], in0=gp[:, kk:kk + T], scalar=col(cv, CV_FCW + f * 3 + kk),
                        in1=cvb[:, 0:T], op0=ALU.mult, op1=ALU.add), reads=kgp + kcb + kcv, writes=kcb)
                P.add("act", lambda e, cvb=cvb: e.activation(out=cvb[:, 0:T], in_=cvb[:, 0:T],
                                                             func=AF.Gelu_apprx_tanh), reads=kcb, writes=kcb)

            def ffn_valmm(c):
                banks = []
                for tb in range(2):
                    b = nbank()
                    mm_group(ps[b][:, :], PK(b), c["lv"], [hB[:, k, tb * TB:(tb + 1) * TB] for k in range(KC)],
                             c["kU"] + hB.k(None, (tb * TB, (tb + 1) * TB)))
                    banks.append(b)
                    held_banks.add(b)
                return banks

            def ffn_mult(c, banks, release=True):
                cvb, fl = c["cvb"], c["fl"]
                for tb in range(2):
                    b = banks[tb]
                    if release:
                        held_banks.discard(b)
                    P.add("dve", lambda e, b=b, cvb=cvb, fl=fl, tb=tb: e.tensor_tensor(
                        out=gv[:, fl, tb * TB:(tb + 1) * TB], in0=cvb[:, tb * TB:(tb + 1) * TB],
                        in1=ps[b][:, :], op=ALU.mult),
                        reads=PK(b) + cvb.k((tb * TB, (tb + 1) * TB)),
                        writes=gv.k(fl, (tb * TB, (tb + 1) * TB)))

            for grp in range(3):
                pr_start = 0
                if grp == 0:
                    c0, c1 = pair_ctx(0, 0)
                    c2, c3 = pair_ctx(0, 1, alt=True)
                    ffn_gate(c0)
                    bk0 = ffn_valmm(c0)
                    ffn_gate(c1)
                    bk1 = ffn_valmm(c1)
                    ffn_gate(c2)
                    ffn_gate(c3)
                    ffn_halo(c0)
                    ffn_post(c0)
                    ffn_mult(c0, bk0, release=False)
                    ffn_halo(c1)
                    ffn_post(c1)
                    ffn_mult(c1, bk1, release=False)
                    for c in (c2, c3):
                        ffn_halo(c)
                        ffn_post(c)
                        ffn_mult(c, ffn_valmm(c))
                    held_banks.difference_update(bk0 + bk1)
                    pr_start = 2
                for pr in range(pr_start, 8):
                    for c in pair_ctx(grp, pr):
                        ffn_gate(c)
                        ffn_halo(c)
                        ffn_post(c)
                        ffn_mult(c, ffn_valmm(c))
                last = (grp == 2)
                nsteps = st0_steps(r + 1) if (last and r + 1 < R) else []
                for cb in range(4):
                    sD = load_unit([(0, wsrc(w_down, grp * 2048, 16, cb * 512, 512), 16, 512)])
                    wd = wv(sD, 0, 16, 512)
                    kD = slots[sD].k()
                    fuse = last and cb == 3
                    order = [6, 7, 0, 1, 2, 3, 4, 5] if fuse else list(range(NT))
                    prev = None
                    for tt in order:
                        b = nbank()
                        tr = (tt * 128, (tt + 1) * 128)
                        mm_group(ps[b][:, :], PK(b), [gv[:, k, tt * 128:(tt + 1) * 128] for k in range(KC)],
                                 [wd[:, k, :] for k in range(KC)], kD + gv.k(None, tr))
                        kx = xres.k(tt, (cb * 512, (cb + 1) * 512))
                        P.add("dve", lambda e, b=b, tt=tt, cb=cb: e.tensor_tensor(
                            out=xres[:, tt, cb * 512:(cb + 1) * 512], in0=ps[b][:, :],
                            in1=xres[:, tt, cb * 512:(cb + 1) * 512], op=ALU.add), reads=PK(b) + kx, writes=kx)
                        if fuse:
                            ctx = final_A(tt)
                            if prev is not None:
                                final_B(prev)
                            prev = ctx
                        elif nsteps and tt % 2 == 0:
                            nsteps.pop(0)()
                    if last and cb == 2:
                        while nsteps:
                            nsteps.pop(0)()
                        if r + 1 < R:
                            st0_done.add(r + 1)
                    if fuse:
                        final_B(prev)

        fin_o0, fin_o1 = P.dsem_val["o0"], P.dsem_val["o1"]

        def fin(e):
            e.wait_ge(dsems["o0"], fin_o0)
            e.wait_ge(dsems["o1"], fin_o1)
            return e.nop()
        P.add("sp", fin)

        P.finalize()

        @block.tensor
        def _(e):
            P.emit("pe", e, engsem, dsems)

        @block.scalar
        def _(e):
            P.emit("act", e, engsem, dsems)

        @block.vector
        def _(e):
            P.emit("dve", e, engsem, dsems)

        @block.gpsimd
        def _(e):
            P.emit("pool", e, engsem, dsems)

        @block.sync
        def _(e):
            P.emit("sp", e, engsem, dsems)
    return nc


def _fm(v, nch):
    return np.ascontiguousarray(np.asarray(v, np.float32).reshape(nch, 128).T)


_NC_CACHE = {}


def kernel(x, g_mix, w_in, b_gate, w_pool, pool_scale, lru_conv_w, lru_conv_b, w_a, b_a, w_i, b_i, lru_lambda,
           w_pool_proj, w_lru_proj, w_out, g_mlp, w_up, ffn_conv_w, ffn_conv_b, w_down, g_final):
    x = np.asarray(x, np.float32)
    B, S, _ = x.shape
    f = lambda a: np.ascontiguousarray(np.asarray(a, np.float32))
    cvb = np.zeros((128, NCV), np.float32)
    cvb[:, CV_GMIX:CV_GMIX + 16] = _fm(g_mix[0], 16)
    cvb[:, CV_GMLP:CV_GMLP + 16] = _fm(g_mlp[0], 16)
    cvb[:, CV_BGATE:CV_BGATE + 32] = _fm(b_gate[0], 32)
    cvb[:, CV_PSCALE:CV_PSCALE + 8] = _fm(pool_scale[0], 8)
    lcw = np.asarray(lru_conv_w[0], np.float32)
    cvb[:, CV_LCW:CV_LCW + 64] = lcw.reshape(4, 16, 128).transpose(2, 1, 0).reshape(128, 64)
    cvb[:, CV_LCB:CV_LCB + 16] = _fm(lru_conv_b[0], 16)
    cvb[:, CV_BA:CV_BA + 16] = _fm(b_a[0], 16)
    cvb[:, CV_BI:CV_BI + 16] = _fm(b_i[0], 16)
    cvb[:, CV_LAM:CV_LAM + 16] = _fm(lru_lambda[0], 16)
    fcw = np.asarray(ffn_conv_w[0], np.float32)
    cvb[:, CV_FCW:CV_FCW + 144] = fcw.reshape(3, 48, 128).transpose(2, 1, 0).reshape(128, 144)
    cvb[:, CV_FCB:CV_FCB + 48] = _fm(ffn_conv_b[0], 48)
    gfb = np.ascontiguousarray(np.broadcast_to(np.asarray(g_final, np.float32)[None, :], (128, D)))
    shared = {
        "gfin": gfb, "w_in": f(w_in[0]), "w_pool": f(w_pool[0]).reshape(1024, 256),
        "w_a": f(w_a[0]).reshape(2048, 256), "w_i": f(w_i[0]).reshape(2048, 256),
        "w_pool_proj": f(w_pool_proj[0]), "w_lru_proj": f(w_lru_proj[0]), "w_out": f(w_out[0]),
        "w_up": f(w_up[0]), "w_down": f(w_down[0]),
    }
    in_maps = []
    for c in range(NCORES):
        b, q = c // 4, c % 4
        xhc = np.zeros((R, T + HALO, D), np.float32)
        cvc = cvb.copy()
        cvc[:, CV_SEL + q] = 1.0
        for r in range(R):
            p = 4 * r + q
            t0 = p * T
            xhc[r, HALO:] = x[b, t0:t0 + T]
            if t0 > 0:
                xhc[r, :HALO] = x[b, t0 - HALO:t0]
            for g, w in enumerate(POOL_W):
                cnt = np.minimum(np.arange(t0 + 1, t0 + 17), w).astype(np.float32)
                cvc[:, CV_INVC + (r * 4 + g) * 16:CV_INVC + (r * 4 + g) * 16 + 16] = (1.0 / cnt)[None, :]
        m = dict(shared)
        m["xh"] = xhc
        m["cv"] = cvc
        in_maps.append(m)
    if "nc" not in _NC_CACHE:
        _NC_CACHE["nc"] = build_nc()
    nc = _NC_CACHE["nc"]
    res = run_bass_kernel_spmd(nc, in_maps, core_ids=list(range(NCORES)))
    out = np.empty((B, S, D), np.float32)
    for c in range(NCORES):
        b, q = c // 4, c % 4
        oc = np.asarray(res.results[c]["out"]).reshape(R, T, D)
        for r in range(R):
            p = 4 * r + q
            out[b, p * T:(p + 1) * T] = oc[r]
    return out
```

```python
import numpy as np
from contextlib import ExitStack
import concourse.bass as bass
import concourse.mybir as mybir
from concourse.bass_utils import run_bass_kernel_spmd

F32 = mybir.dt.float32
BF16 = mybir.dt.bfloat16
AF = mybir.ActivationFunctionType
ALU = mybir.AluOpType

NCORES = 8
R = 2
T = 1024
HALO = 16
D = 2048
KC = 16
NT = 8
TB = 512
DFF = 6144
EPS = 1e-6
POOL_W = (2, 4, 8, 16)

CV_GMIX, CV_GMLP, CV_BGATE, CV_PSCALE, CV_LCW, CV_LCB, CV_BA, CV_BI, CV_LAM, CV_FCW, CV_FCB, CV_SEL, CV_INVC = (
    0, 16, 32, 64, 72, 136, 152, 168, 184, 200, 344, 392, 396)
NCV = 396 + R * 4 * 16

BASE = 16512
OFF_A, OFF_B, OFF_C, OFF_RING, OFF_F = 0, 65536, 98304, 131072, 163840
NSLOT = 2
SLOT_ELEMS = 8192
UNIT = 256


class View:
    def __init__(self, nc, name, fshape, dt, off):
        self.es = 2 if dt == BF16 else 4
        self.t = nc.alloc_sbuf_tensor_at(name, [128] + list(fshape), dt, offset=BASE + off)
        self.off = off
        self.fshape = tuple(fshape)
        st = [1] * len(fshape)
        for i in range(len(fshape) - 2, -1, -1):
            st[i] = st[i + 1] * fshape[i + 1]
        self.st = st
        self._cache = {}
        assert off + self.es * int(np.prod(fshape)) <= 212832, name

    def __getitem__(self, idx):
        return self.t[idx]

    def k(self, *idx):
        if idx in self._cache:
            return self._cache[idx]
        rg = []
        for d, n in enumerate(self.fshape):
            i = idx[d] if d < len(idx) else None
            if i is None:
                rg.append((0, n))
            elif isinstance(i, int):
                rg.append((i, i + 1))
            else:
                rg.append(i)
        keys = set()

        def rec(d, base):
            if d == len(rg) - 1:
                b0 = self.off + (base + rg[d][0]) * self.es
                b1 = self.off + (base + rg[d][1]) * self.es
                keys.update(range(b0 // UNIT, (b1 + UNIT - 1) // UNIT))
            else:
                for i in range(rg[d][0], rg[d][1]):
                    rec(d + 1, base + i * self.st[d])
        rec(0, 0)
        res = tuple(keys)
        self._cache[idx] = res
        return res


class Op:
    __slots__ = ("eng", "fn", "cdeps", "ddeps", "kind", "dsem", "val", "inc", "lidx", "sig")


ENGS = ("pe", "act", "dve", "pool", "sp")


class Prog:
    def __init__(self):
        self.eng_ops = {e: [] for e in ENGS}
        self.lastw = {}
        self.readers = {}
        self.dsem_val = {}

    def add(self, eng, fn, reads=(), writes=(), kind="c", dsem=None, inc=16):
        op = Op()
        op.eng, op.fn, op.kind, op.dsem, op.inc = eng, fn, kind, dsem, inc
        op.lidx = len(self.eng_ops[eng])
        op.sig = False
        cdeps = {}
        ddeps = {}

        def dep(o, hazard):
            if o.kind == "c":
                if o.eng == eng and kind == "c":
                    if eng == "pe":
                        return
                cur = cdeps.get(o.eng)
                if cur is None or o.lidx > cur:
                    cdeps[o.eng] = o.lidx
            else:
                cur = ddeps.get(o.dsem)
                if cur is None or o.val > cur:
                    ddeps[o.dsem] = o.val
        lastw, readers = self.lastw, self.readers
        for k in reads:
            w = lastw.get(k)
            if w is not None:
                dep(w, True)
        for k in writes:
            w = lastw.get(k)
            if w is not None:
                dep(w, True)
            rl = readers.get(k)
            if rl:
                for o in rl.values():
                    dep(o, False)
        for k in reads:
            rl = readers.get(k)
            if rl is None:
                rl = readers[k] = {}
            rl[(eng, dsem) if kind != "c" else eng] = op
        for k in writes:
            lastw[k] = op
            readers[k] = {}
        if kind != "c":
            v = self.dsem_val.get(dsem, 0) + inc
            self.dsem_val[dsem] = v
            op.val = v
        op.cdeps, op.ddeps = cdeps, ddeps
        self.eng_ops[eng].append(op)
        return op

    def finalize(self):
        for e in ENGS:
            for op in self.eng_ops[e]:
                for de, li in op.cdeps.items():
                    self.eng_ops[de][li].sig = True
        self.sigval = {}
        for e in ENGS:
            cnt = 0
            vals = []
            for op in self.eng_ops[e]:
                if op.kind == "c" and op.sig:
                    cnt += 1
                vals.append(cnt)
            self.sigval[e] = vals

    def emit(self, e, engine, engsem, dsems):
        waited = {}
        for op in self.eng_ops[e]:
            for de, li in op.cdeps.items():
                v = self.sigval[de][li]
                key = ("e", de)
                if waited.get(key, 0) < v:
                    engine.wait_ge(engsem[de], v)
                    waited[key] = v
            for ds, v in op.ddeps.items():
                key = ("d", ds)
                if waited.get(key, 0) < v:
                    engine.wait_ge(dsems[ds], v)
                    waited[key] = v
            ins = op.fn(engine)
            if op.kind == "c":
                if op.sig:
                    ins.then_inc(engsem[e], 1)
            else:
                ins.then_inc(dsems[op.dsem], op.inc)


def build_nc():
    nc = bass.Bass("TRN2", target_bir_lowering=False)
    P = Prog()

    def dram(name, shape, kind="ExternalInput"):
        return nc.dram_tensor(name, list(shape), F32, kind=kind).ap()

    xh = dram("xh", [R, T + HALO, D])
    cvd = dram("cv", [128, NCV])
    gfd = dram("gfin", [128, D])
    w_in = dram("w_in", [D, 9216])
    w_pool = dram("w_pool", [1024, 256])
    w_a = dram("w_a", [2048, 256])
    w_i = dram("w_i", [2048, 256])
    w_pp = dram("w_pool_proj", [1024, D])
    w_lp = dram("w_lru_proj", [2048, D])
    w_out = dram("w_out", [D, D])
    w_up = dram("w_up", [D, 2 * DFF])
    w_down = dram("w_down", [DFF, D])
    outd = dram("out", [R, T, D], kind="ExternalOutput")
    ccin = [dram(f"ccin{u}", [128, 32], kind="Internal") for u in range(2 * R)]
    ccout = [dram(f"ccout{u}", [4 * 128, 32], kind="Internal") for u in range(2 * R)]

    es = ExitStack()
    with es:
        assert nc.sbuf_base <= BASE
        arena = es.enter_context(nc.sbuf_tensor("arena", [128, (229344 - BASE) // 4], F32))
        V = lambda name, fshape, dt, off: View(nc, name, fshape, dt, off)
        ylru = V("ylru", [16, T], BF16, OFF_A)
        ypool = V("ypool", [8, T], BF16, OFF_A + 32768)
        xres = V("xres", [NT, D], F32, OFF_A)
        hB = V("hB", [16, T], BF16, OFF_B)
        p2 = V("p2", [16, T], BF16, OFF_C)
        slots = [V(f"slot{s}", [SLOT_ELEMS], BF16, OFF_RING + s * 16384) for s in range(NSLOT)]
        gfin = V("gfin_sb", [D], F32, OFF_F)
        cv = V("cv_sb", [NCV], F32, OFF_F + 8192)
        drv = V("drv", [96], F32, OFF_F + 10304)
        ident = V("ident", [128], F32, OFF_F + 10688)
        sm = V("sm", [256], F32, OFF_F + 11200)
        hhalo = V("hhalo", [16, HALO], BF16, OFF_F + 12224)
        h2halo = V("h2halo", [16, 2], BF16, OFF_F + 12736)
        ccg = V("ccg1", [4, 32], F32, OFF_F + 12800)
        ccg2 = V("ccg2", [4, 32], F32, OFF_F + 13312)
        ident_bf = V("ident_bf", [128], BF16, OFF_F + 13824)
        OFF_T = OFF_F + 14080
        TSZ = 4224
        tf = [V(f"tf{i}", [1056], F32, OFF_T + i * TSZ) for i in range(8)]
        xns = [V("xn%d" % i, [D], BF16, OFF_T + (4 + i) * TSZ) for i in range(4)]
        vv = V("vv", [2, T], F32, OFF_A + 49152)
        vb = V("vb", [2, T], BF16, OFF_A + 49152 + 8192)
        dbuf = V("dbuf", [2, T], BF16, OFF_A + 49152 + 12288)
        gl1 = V("gl1", [T], F32, OFF_A + 49152 + 12288)
        dbufB = V("dbufB", [2, T], BF16, OFF_A + 49152 + 8192)
        dbufs = [dbuf, dbufB]
        wpool_sb = V("wpool_sb", [8, 256], BF16, OFF_A + 49152)
        ex = [V("ex0", [T], F32, OFF_A + 12 * 2048), V("ex1", [T], F32, OFF_A + 14 * 2048),
              V("ex2", [T], F32, OFF_C + 12 * 2048), V("ex3", [T], F32, OFF_C + 14 * 2048)]
        ost = [V("ost0", [D], F32, OFF_T), V("ost1", [D], F32, OFF_T + 2 * TSZ)]
        xst = V("xst", [2, D], F32, OFF_T)

        DV_HCL, DV_HBA, DV_HBI, DV_E, DV_SP, DV_TMP = 0, 16, 32, 48, 64, 80
        SM_SS, SM_SD, SM_RS = 0, 16, 32
        SM_CC1, SM_HIN, SM_SC, SM_CC2, SM_HH, SM_T16, SM_GC = 48, 80, 96, 112, 144, 176, 192
        SM_KEYS = tuple(("sm", i) for i in range(48)) + tuple(("sm", n) for n in ("cc1", "hin", "sc", "cc2", "hh", "t16", "gc"))

        ps = [es.enter_context(nc.psum_tensor(f"ps{b}", [128, 512], F32)) for b in range(8)]
        engsem = {e: es.enter_context(nc.semaphore(f"sem_{e}")) for e in ENGS}
        dsem_names = (["slot%d" % s for s in range(NSLOT)] + ["c0", "c1", "x0", "x1", "xs0", "xs1", "xs2", "xs3",
                      "o0", "o1", "cci", "ccip", "cc", "ccb", "wp"])
        dsems = {n: es.enter_context(nc.semaphore("ds_" + n)) for n in dsem_names}
        block = es.enter_context(nc.Block())

        bank_ctr = [0]

        held_banks = set()

        def nbank():
            while True:
                b = bank_ctr[0] % 8
                bank_ctr[0] += 1
                if b not in held_banks:
                    return b

        def PK(b):
            return (("ps", b),)

        def col(vw, c, n=1):
            return vw[:, c:c + n]

        P.add("sp", lambda e: e.dma_start(out=cv[:, :], in_=cvd), writes=cv.k(), kind="dma", dsem="c0")
        P.add("sp", lambda e: e.dma_start(out=gfin[:, :], in_=gfd), writes=gfin.k(), kind="dma", dsem="c1")

        P.add("pool", lambda e: e.memset(ident[:, :], 0.0), writes=ident.k())
        P.add("pool", lambda e: e.affine_select(out=ident[:, :], in_=ident[:, :], pattern=[[-1, 128]],
                                                compare_op=ALU.not_equal, fill=1.0, base=0, channel_multiplier=1),
              reads=ident.k(), writes=ident.k())

        P.add("dve", lambda e: e.tensor_copy(out=ident_bf[:, :], in_=ident[:, :]), reads=ident.k(), writes=ident_bf.k())
        kcv = cv.k()
        kdrv = drv.k()
        P.add("act", lambda e: e.activation(out=col(drv, DV_E, 16), in_=col(cv, CV_LAM, 16), func=AF.Exp, scale=-1.0),
              reads=kcv, writes=kdrv)
        P.add("dve", lambda e: e.tensor_scalar(out=col(drv, DV_TMP, 16), in0=col(drv, DV_E, 16), scalar1=-0.25,
                                               scalar2=1.0 / 3.0, op0=ALU.mult, op1=ALU.add), reads=kdrv, writes=kdrv)
        P.add("dve", lambda e: e.tensor_tensor(out=col(drv, DV_TMP, 16), in0=col(drv, DV_TMP, 16),
                                               in1=col(drv, DV_E, 16), op=ALU.mult), reads=kdrv, writes=kdrv)
        P.add("dve", lambda e: e.tensor_scalar(out=col(drv, DV_TMP, 16), in0=col(drv, DV_TMP, 16), scalar1=-1.0,
                                               scalar2=0.5, op0=ALU.mult, op1=ALU.add), reads=kdrv, writes=kdrv)
        P.add("dve", lambda e: e.tensor_tensor(out=col(drv, DV_TMP, 16), in0=col(drv, DV_TMP, 16),
                                               in1=col(drv, DV_E, 16), op=ALU.mult), reads=kdrv, writes=kdrv)
        P.add("dve", lambda e: e.tensor_scalar(out=col(drv, DV_TMP, 16), in0=col(drv, DV_TMP, 16), scalar1=-1.0,
                                               scalar2=1.0, op0=ALU.mult, op1=ALU.add), reads=kdrv, writes=kdrv)
        P.add("dve", lambda e: e.tensor_tensor(out=col(drv, DV_SP, 16), in0=col(drv, DV_TMP, 16),
                                               in1=col(drv, DV_E, 16), op=ALU.mult), reads=kdrv, writes=kdrv)
        P.add("dve", lambda e: e.tensor_scalar(out=col(drv, DV_HCL, 16), in0=col(drv, DV_SP, 16), scalar1=-4.0,
                                               scalar2=None, op0=ALU.mult), reads=kdrv, writes=kdrv)
        P.add("dve", lambda e: e.tensor_scalar(out=col(drv, DV_HBA, 16), in0=col(cv, CV_BA, 16), scalar1=0.5,
                                               scalar2=None, op0=ALU.mult), reads=kcv, writes=kdrv)
        P.add("dve", lambda e: e.tensor_scalar(out=col(drv, DV_HBI, 16), in0=col(cv, CV_BI, 16), scalar1=0.5,
                                               scalar2=None, op0=ALU.mult), reads=kcv, writes=kdrv)
        P.add("dve", lambda e: e.memset(sm[:, :], 0.0), writes=sm.k() + SM_KEYS)

        ring_ctr = [0]

        def load_unit(parts):
            s = ring_ctr[0] % NSLOT
            ring_ctr[0] += 1
            sl = slots[s]
            ops = []
            for i, (eo, src, kc_, n_) in enumerate(parts):
                def f(e, eo=eo, src=src, kc_=kc_, n_=n_, sl=sl):
                    dst = sl[:, eo:eo + kc_ * n_].rearrange("p (k n) -> p k n", n=n_)
                    return e.dma_start(out=dst, in_=src)
                ops.append(P.add("pool", f, writes=sl.k((eo, eo + kc_ * n_)), kind="dma", dsem="slot%d" % s))
            for o in ops:
                o.val = ops[-1].val
            return s

        def wsrc(w, r0, nk, c0, n):
            return w[r0:r0 + nk * 128, c0:c0 + n].rearrange("(k p) n -> p k n", p=128)

        def wv(s, eo, kc_, n_):
            return slots[s][:, eo:eo + kc_ * n_].rearrange("p (k n) -> p k n", n=n_)

        def mm_group(out_ap, pskeys, lhs_list, rhs_list, reads):
            def f(e):
                n = len(lhs_list)
                ins = None
                for i in range(n):
                    ins = e.matmul(out_ap, lhsT=lhs_list[i], rhs=rhs_list[i], start=(i == 0), stop=(i == n - 1))
                return ins
            return P.add("pe", f, reads=reads, writes=pskeys)

        rot = {"ss": 0, "x": 0, "xs": 0, "o": 0, "xn": 0}

        def norm_A(src_ap, src_keys, npart):
            sc = rot["ss"] % 16
            rot["ss"] += 1
            xn = xns[rot["xn"] % 4]
            rot["xn"] += 1
            kss, ksd, krs = (("sm", SM_SS + sc),), (("sm", SM_SD + sc),), (("sm", SM_RS + sc),)
            ss = sm[0:npart, SM_SS + sc:SM_SS + sc + 1]
            sd = sm[0:npart, SM_SD + sc:SM_SD + sc + 1]
            rs = sm[0:npart, SM_RS + sc:SM_RS + sc + 1]
            P.add("act", lambda e: e.activation(out=xn[0:npart, :], in_=src_ap, func=AF.Square, accum_out=ss),
                  reads=src_keys, writes=xn.k() + kss)
            P.add("act", lambda e: e.activation(out=sd, in_=ss, func=AF.Sqrt, scale=1.0 / D, bias=EPS),
                  reads=kss, writes=ksd)
            P.add("dve", lambda e: e.reciprocal(out=rs, in_=sd), reads=ksd, writes=krs)
            P.add("act", lambda e: e.activation(out=xn[0:npart, :], in_=src_ap, func=AF.Copy, scale=rs),
                  reads=src_keys + krs, writes=xn.k())
            return (xn, npart)

        def norm_B(ctx, gcol, dst_view, dst_col0, ncols):
            xn, npart = ctx
            for g4 in range(4):
                b = nbank()
                pv = ps[b][:, :].bitcast(BF16).rearrange("p (a t) -> p a t", t=128)[:, 0:4, :]

                def ft(e, g4=g4, pv=pv, xn=xn):
                    ins = None
                    for j in range(4):
                        c = g4 * 4 + j
                        ins = e.transpose(out=pv[:, j, 0:npart], in_=xn[0:npart, c * 128:(c + 1) * 128],
                                          identity=ident_bf[0:npart, 0:npart])
                    return ins
                P.add("pe", ft, reads=xn.k((g4 * 512, g4 * 512 + 512)) + ident_bf.k(), writes=PK(b))

                def fe(e, g4=g4, pv=pv):
                    gsl = cv[:, gcol + g4 * 4:gcol + g4 * 4 + 4]
                    return e.tensor_tensor(out=dst_view[:, g4 * 4:g4 * 4 + 4, dst_col0:dst_col0 + ncols],
                                           in0=pv[:, :, 0:npart],
                                           in1=gsl.unsqueeze(2).to_broadcast([128, 4, npart]), op=ALU.mult)
                P.add("dve", fe, reads=PK(b) + kcv,
                      writes=dst_view.k((g4 * 4, g4 * 4 + 4), (dst_col0, dst_col0 + ncols)))

        def norm_to_fm(src_ap, src_keys, npart, gcol, dst_view, dst_col0, ncols):
            norm_B(norm_A(src_ap, src_keys, npart), gcol, dst_view, dst_col0, ncols)

        cc_u = [0]

        pending_rb = []

        def exchange_readback():
            u, dst_view = pending_rb.pop()
            P.add("sp", lambda e: e.dma_start(out=dst_view[:, :, :], in_=ccout[u].rearrange("(c p) n -> p c n", p=128)),
                  reads=(("ccout", u),), writes=dst_view.k(), kind="dma", dsem="ccb")

        def exchange(src_ap, src_keys, dst_view, in_queue="sp", defer_readback=False):
            u = cc_u[0]
            cc_u[0] += 1
            kin, kout = (("ccin", u),), (("ccout", u),)
            P.add(in_queue, lambda e: e.dma_start(out=ccin[u], in_=src_ap), reads=src_keys, writes=kin,
                  kind="dma", dsem="cci" if in_queue == "sp" else "ccip")
            P.add("pool", lambda e: e.collective_compute("AllGather", ALU.bypass,
                                                         replica_groups=[[0, 1, 2, 3], [4, 5, 6, 7]],
                                                         ins=[ccin[u]], outs=[ccout[u]]),
                  reads=kin, writes=kout, kind="cc", dsem="cc", inc=1)
            pending_rb.append((u, dst_view))
            if not defer_readback:
                exchange_readback()

        st0_done = set()

        def st0_steps(r):
            state = {"prevB": None}
            steps = []
            for ti in range(-1, NT):
                def step(ti=ti):
                    par = rot["x"] % 2
                    rot["x"] += 1
                    if ti < 0:
                        npart, src = HALO, xh[r, 0:HALO, :]
                    else:
                        npart, src = 128, xh[r, HALO + ti * 128:HALO + (ti + 1) * 128, :]
                    P.add("sp", lambda e, par=par, npart=npart, src=src: e.dma_start(out=xst[0:npart, par, :], in_=src),
                          writes=xst.k(par), kind="dma", dsem="x%d" % par)
                    ctx = norm_A(xst[0:npart, par, :], xst.k(par), npart)
                    if state["prevB"] is not None:
                        norm_B(*state["prevB"])
                    state["prevB"] = (ctx, CV_GMIX, hhalo, 0, HALO) if ti < 0 else (ctx, CV_GMIX, hB, ti * 128, 128)
                steps.append(step)
            steps.append(lambda: norm_B(*state["prevB"]))
            return steps

        cur_round = [0]

        def final_A(tt):
            sc = rot["ss"] % 16
            rot["ss"] += 1
            kss, ksd = (("sm", SM_SS + sc),), (("sm", SM_SD + sc),)
            ss, sd = col(sm, SM_SS + sc), col(sm, SM_SD + sc)
            oi = rot["o"] % 2
            rot["o"] += 1
            ob = ost[oi]
            P.add("act", lambda e, tt=tt, ss=ss, ob=ob: e.activation(out=ob[:, :], in_=xres[:, tt, :], func=AF.Square,
                                                                     accum_out=ss), reads=xres.k(tt), writes=ob.k() + kss)
            P.add("act", lambda e, ss=ss, sd=sd: e.activation(out=sd, in_=ss, func=AF.Sqrt, scale=1.0 / D, bias=EPS),
                  reads=kss, writes=ksd)
            return (tt, sc, oi)

        def final_B(ctx):
            tt, sc, oi = ctx
            r = cur_round[0]
            ksd, krs = (("sm", SM_SD + sc),), (("sm", SM_RS + sc),)
            sd, rs = col(sm, SM_SD + sc), col(sm, SM_RS + sc)
            ob = ost[oi]
            P.add("dve", lambda e, sd=sd, rs=rs: e.reciprocal(out=rs, in_=sd), reads=ksd, writes=krs)
            P.add("dve", lambda e, tt=tt, rs=rs, ob=ob: e.scalar_tensor_tensor(
                out=ob[:, :], in0=xres[:, tt, :], scalar=rs, in1=gfin[:, :], op0=ALU.mult, op1=ALU.mult),
                reads=xres.k(tt) + krs + gfin.k(), writes=ob.k())
            P.add("sp", lambda e, tt=tt, ob=ob, r=r: e.dma_start(out=outd[r, tt * 128:(tt + 1) * 128, :], in_=ob[:, :]),
                  reads=ob.k(), writes=(("out", r, tt),), kind="dma", dsem="o%d" % oi)

        for r in range(R):
            cur_round[0] = r
            if r not in st0_done:
                for st in st0_steps(r):
                    st()

            zt = tf[7]
            P.add("dve", lambda e: e.memset(zt[:, 0:T], 0.0), writes=zt.k((0, T)))

            ul, Hb, Ab, gl0 = tf[0], tf[4], tf[5], tf[6]
            gls = [gl0, gl1]

            def lru_load(hb):
                sA = load_unit([(0, wsrc(w_in, 0, 16, 1024 + hb * 256, 256), 16, 256)])
                sB = load_unit([(0, wsrc(w_in, 0, 16, 3072 + hb * 256, 256), 16, 256),
                                (4096, wsrc(w_a, hb * 256, 2, 0, 256), 2, 256),
                                (4608, wsrc(w_i, hb * 256, 2, 0, 256), 2, 256)])
                if hb <= 5:
                    bsets = [(tf[1], tf[2], tf[3]), (ex[0], ex[1], ex[2])]
                elif hb == 6:
                    bsets = [(tf[1], tf[2], tf[3]), (ex[1], ex[3], tf[3])]
                else:
                    bsets = [(tf[1], tf[2], tf[3]), (tf[1], tf[2], tf[3])]
                return dict(hb=hb, wl=wv(sA, 0, 16, 256), wa=wv(sB, 4096, 2, 256), wi=wv(sB, 4608, 2, 256),
                            wg=wv(sB, 0, 16, 256), kA_l=slots[sA].k((0, 4096)), kA_g=slots[sB].k((4096, 5120)),
                            kB=slots[sB].k((0, 4096)), bsets=bsets,
                            gls=[gl0, ex[3] if (hb <= 5 and hb % 2 == 1) else gl1])

            def lru_X(cx, ci):
                hb, wl, kA_l = cx["hb"], cx["wl"], cx["kA_l"]
                c = 2 * hb + ci
                segs = [(None, HALO, 0)] + [(tb, TB, HALO + tb * TB) for tb in range(2)]
                for (tb, n, uoff) in segs:
                    b = nbank()
                    if tb is None:
                        rhs = [hhalo[:, k, :] for k in range(KC)]
                        rk = hhalo.k()
                    else:
                        rhs = [hB[:, k, tb * TB:(tb + 1) * TB] for k in range(KC)]
                        rk = hB.k(None, (tb * TB, (tb + 1) * TB))
                    mm_group(ps[b][:, 0:n], PK(b), [wl[:, k, ci * 128:(ci + 1) * 128] for k in range(KC)], rhs,
                             kA_l + rk)
                    P.add("act", lambda e, b=b, n=n, uoff=uoff: e.activation(out=ul[:, uoff:uoff + n],
                                                                             in_=ps[b][:, 0:n], func=AF.Copy),
                          reads=PK(b), writes=ul.k((uoff, uoff + n)))
                P.add("act", lambda e, c=c, ci=ci: e.activation(out=vv[:, ci, :], in_=ul[:, HALO:HALO + T],
                                                                func=AF.Identity,
                                                                scale=col(cv, CV_LCW + c * 4 + 3),
                                                                bias=col(cv, CV_LCB + c)),
                      reads=ul.k() + kcv, writes=vv.k(ci))
                for kk in range(3):
                    P.add("dve", lambda e, c=c, ci=ci, kk=kk: e.scalar_tensor_tensor(
                        out=vv[:, ci, :], in0=ul[:, HALO - 3 + kk:HALO - 3 + kk + T],
                        scalar=col(cv, CV_LCW + c * 4 + kk), in1=vv[:, ci, :], op0=ALU.mult, op1=ALU.add),
                        reads=ul.k() + kcv + vv.k(ci), writes=vv.k(ci))
                P.add("act", lambda e, ci=ci: e.activation(out=vb[:, ci, :], in_=vv[:, ci, :], func=AF.Copy),
                      reads=vv.k(ci), writes=vb.k(ci))

            def lru_G(cx):
                wg, kB = cx["wg"], cx["kB"]
                for ci in range(2):
                    gl = cx["gls"][ci]
                    for tb in range(2):
                        b = nbank()
                        mm_group(ps[b][:, :], PK(b), [wg[:, k, ci * 128:(ci + 1) * 128] for k in range(KC)],
                                 [hB[:, k, tb * TB:(tb + 1) * TB] for k in range(KC)],
                                 kB + hB.k(None, (tb * TB, (tb + 1) * TB)))
                        P.add("act", lambda e, b=b, tb=tb, gl=gl: e.activation(out=gl[:, tb * TB:(tb + 1) * TB],
                                                                               in_=ps[b][:, :], func=AF.Gelu_apprx_tanh),
                              reads=PK(b), writes=gl.k((tb * TB, (tb + 1) * TB)))

            def gates(cx, ci):
                hb, wa, wi, kA_g = cx["hb"], cx["wa"], cx["wi"], cx["kA_g"]
                c = 2 * hb + ci
                thr, thi, mt = cx["bsets"][ci]
                for (wmat, bcol, dstv) in ((wa, DV_HBA, thr), (wi, DV_HBI, thi)):
                    for tb in range(2):
                        b = nbank()
                        mm_group(ps[b][:, :], PK(b), [wmat[:, k, ci * 128:(ci + 1) * 128] for k in range(2)],
                                 [vb[:, k, tb * TB:(tb + 1) * TB] for k in range(2)],
                                 kA_g + vb.k(None, (tb * TB, (tb + 1) * TB)))
                        P.add("act", lambda e, b=b, tb=tb, dstv=dstv, bcol=bcol, c=c: e.activation(
                            out=dstv[:, tb * TB:(tb + 1) * TB], in_=ps[b][:, :], func=AF.Tanh, scale=0.5,
                            bias=col(drv, bcol + c)),
                            reads=PK(b) + kdrv, writes=dstv.k((tb * TB, (tb + 1) * TB)))

            def act_exp(cx, ci):
                c = 2 * cx["hb"] + ci
                thr, thi, mt = cx["bsets"][ci]
                kthr = thr.k((0, T))
                P.add("act", lambda e, c=c, thr=thr: e.activation(out=thr[:, 0:T], in_=thr[:, 0:T], func=AF.Exp,
                                                                  scale=col(drv, DV_HCL + c),
                                                                  bias=col(drv, DV_HCL + c)),
                      reads=kthr + kdrv, writes=kthr)

            def act_sq(cx, ci):
                thr, thi, mt = cx["bsets"][ci]
                P.add("act", lambda e, thr=thr, mt=mt: e.activation(out=mt[:, 0:T], in_=thr[:, 0:T], func=AF.Square),
                      reads=thr.k((0, T)), writes=mt.k((0, T)))

            def act_sqrt(cx, ci):
                thr, thi, mt = cx["bsets"][ci]
                P.add("act", lambda e, mt=mt: e.activation(out=mt[:, 0:T], in_=mt[:, 0:T], func=AF.Sqrt, scale=-0.25,
                                                           bias=0.25), reads=mt.k((0, T)), writes=mt.k((0, T)))

            def dve_chain(cx, ci, part):
                c = 2 * cx["hb"] + ci
                thr, thi, mt = cx["bsets"][ci]
                gl = cx["gls"][ci]
                kthr, kthi, kmt = thr.k((0, T)), thi.k((0, T)), mt.k((0, T))
                kH, kAb = Hb.k((0, T)), Ab.k((0, T))
                if part == "tail":
                    kgl = gl.k((0, T))
                    P.add("dve", lambda e, c=c, gl=gl: e.tensor_tensor(out=ylru[:, c, :], in0=Hb[:, 0:T], in1=gl[:, 0:T],
                                                                       op=ALU.mult), reads=kH + kgl, writes=ylru.k(c))
                    P.add("dve", lambda e, c=c, gl=gl: e.tensor_tensor(out=p2[:, c, :], in0=Ab[:, 0:T], in1=gl[:, 0:T],
                                                                       op=ALU.mult), reads=kAb + kgl, writes=p2.k(c))
                    kc1 = (("sm", "cc1"),)
                    P.add("dve", lambda e, c=c: e.tensor_copy(out=col(sm, SM_CC1 + 2 * c), in_=Hb[:, T - 1:T]),
                          reads=kH, writes=kc1)
                    P.add("dve", lambda e, c=c: e.tensor_copy(out=col(sm, SM_CC1 + 2 * c + 1), in_=Ab[:, T - 1:T]),
                          reads=kAb, writes=kc1)
                    return
                if part in ("head", "t"):
                    P.add("dve", lambda e, ci=ci, thi=thi: e.scalar_tensor_tensor(out=thi[:, 0:T], in0=thi[:, 0:T],
                                                                                  scalar=1.0, in1=vv[:, ci, :],
                                                                                  op0=ALU.add, op1=ALU.mult),
                          reads=kthi + vv.k(ci), writes=kthi)
                if part == "t":
                    return
                P.add("dve", lambda e, thi=thi, mt=mt: e.tensor_tensor(out=thi[:, 0:T], in0=thi[:, 0:T], in1=mt[:, 0:T],
                                                                       op=ALU.mult), reads=kthi + kmt, writes=kthi)
                P.add("dve", lambda e, thr=thr, thi=thi: e.tensor_tensor_scan(
                    out=Hb[:, 0:T], data0=thr[:, 0:T], data1=thi[:, 0:T], initial=0.0, op0=ALU.mult, op1=ALU.add),
                    reads=kthr + kthi, writes=kH)
                P.add("dve", lambda e, thr=thr: e.tensor_tensor_scan(
                    out=Ab[:, 0:T], data0=thr[:, 0:T], data1=zt[:, 0:T], initial=1.0, op0=ALU.mult, op1=ALU.add),
                    reads=kthr + zt.k((0, T)), writes=kAb)

            cx = lru_load(0)
            lru_X(cx, 0)
            lru_X(cx, 1)
            lru_G(cx)
            for hb in range(8):
                bsets = cx["bsets"]
                dbl_t = bsets[0][0] is not bsets[1][0]
                dbl_m = bsets[0][2] is not bsets[1][2]
                nxt_box = [None]

                def chain_and_next(ci):
                    if ci == 1 and hb + 1 < 8 and nxt_box[0] is not None:
                        dve_chain(cx, ci, "t")
                        lru_X(nxt_box[0], ci)
                        dve_chain(cx, ci, "mid")
                        dve_chain(cx, ci, "tail")
                    else:
                        dve_chain(cx, ci, "head")
                        tail_and_next(ci)

                def tail_and_next(ci):
                    if hb + 1 < 8:
                        if nxt_box[0] is None:
                            nxt_box[0] = lru_load(hb + 1)
                        lru_X(nxt_box[0], ci)
                    dve_chain(cx, ci, "tail")

                if dbl_t:
                    gates(cx, 0)
                    gates(cx, 1)
                    act_exp(cx, 0)
                    act_exp(cx, 1)
                    if dbl_m:
                        act_sq(cx, 0)
                        act_sq(cx, 1)
                        act_sqrt(cx, 0)
                        act_sqrt(cx, 1)
                        chain_and_next(0)
                        chain_and_next(1)
                    else:
                        act_sq(cx, 0)
                        act_sqrt(cx, 0)
                        chain_and_next(0)
                        act_sq(cx, 1)
                        act_sqrt(cx, 1)
                        chain_and_next(1)
                else:
                    for ci in range(2):
                        gates(cx, ci)
                        act_exp(cx, ci)
                        act_sq(cx, ci)
                        act_sqrt(cx, ci)
                        chain_and_next(ci)
                if nxt_box[0] is not None:
                    lru_G(nxt_box[0])
                    cx = nxt_box[0]

            merged = p2

            def merge_ctx(j):
                sM = load_unit([(0, wsrc(w_pp, 0, 8, j * 128, 128), 8, 128),
                                (1024, wsrc(w_lp, 0, 16, j * 128, 128), 16, 128),
                                (3072, wsrc(w_in, 0, 16, 5120 + j * 128, 128), 16, 128),
                                (5120, wsrc(w_in, 0, 16, 7168 + j * 128, 128), 16, 128)])
                if j == 0:
                    bufs = [(tf[3], tf[4], tf[7], tf[0]), (tf[5], tf[6], tf[1], tf[2])]
                else:
                    bufs = [(tf[0], tf[1], tf[2], tf[3]), (tf[4], tf[5], tf[6], tf[7])]
                return dict(j=j, wpp=wv(sM, 0, 8, 128), wlp=wv(sM, 1024, 16, 128), wg0=wv(sM, 3072, 16, 128),
                            wg1=wv(sM, 5120, 16, 128), kM=slots[sM].k((0, 7168)), bufs=bufs, banks={})

            def merge_g(mc, tb):
                j, kM = mc["j"], mc["kM"]
                tsl = slice(tb * TB, (tb + 1) * TB)
                tr = (tb * TB, (tb + 1) * TB)
                g0t, g1t, m0t, m1t = mc["bufs"][tb]
                for (wg, gt, boff) in ((mc["wg0"], g0t, 0), (mc["wg1"], g1t, 16)):
                    b = nbank()
                    mm_group(ps[b][:, :], PK(b), [wg[:, k, :] for k in range(16)], [hB[:, k, tsl] for k in range(16)],
                             kM + hB.k(None, tr))
                    P.add("act", lambda e, b=b, gt=gt, j=j, boff=boff: e.activation(
                        out=gt[:, 0:TB], in_=ps[b][:, :], func=AF.Sigmoid, bias=col(cv, CV_BGATE + boff + j)),
                        reads=PK(b) + kcv, writes=gt.k((0, TB)))

            def merge_P(mc, tb):
                kM = mc["kM"]
                tsl = slice(tb * TB, (tb + 1) * TB)
                tr = (tb * TB, (tb + 1) * TB)
                g0t, g1t, m0t, m1t = mc["bufs"][tb]
                bP = nbank()
                mm_group(ps[bP][:, :], PK(bP), [mc["wpp"][:, k, :] for k in range(8)], [ypool[:, k, tsl] for k in range(8)],
                         kM + ypool.k(None, tr))
                P.add("dve", lambda e, bP=bP, g0t=g0t, m0t=m0t: e.tensor_tensor(out=m0t[:, 0:TB], in0=g0t[:, 0:TB],
                                                                                in1=ps[bP][:, :], op=ALU.mult),
                      reads=PK(bP) + g0t.k((0, TB)), writes=m0t.k((0, TB)))

            def merge_p2(mc, tb):
                j, kM = mc["j"], mc["kM"]
                tsl = slice(tb * TB, (tb + 1) * TB)
                tr = (tb * TB, (tb + 1) * TB)
                g0t, g1t, m0t, m1t = mc["bufs"][tb]
                bL = nbank()
                mm_group(ps[bL][:, :], PK(bL), [mc["wlp"][:, k, :] for k in range(16)], [ylru[:, k, tsl] for k in range(16)],
                         kM + ylru.k(None, tr))
                P.add("dve", lambda e, bL=bL, g1t=g1t, m1t=m1t: e.tensor_tensor(out=m1t[:, 0:TB], in0=g1t[:, 0:TB],
                                                                                in1=ps[bL][:, :], op=ALU.mult),
                      reads=PK(bL) + g1t.k((0, TB)), writes=m1t.k((0, TB)))
                P.add("dve", lambda e, m0t=m0t, m1t=m1t, j=j, tsl=tsl: e.tensor_tensor(
                    out=merged[:, j, tsl], in0=m0t[:, 0:TB], in1=m1t[:, 0:TB], op=ALU.add),
                    reads=m0t.k((0, TB)) + m1t.k((0, TB)), writes=merged.k(j, tr))

            up, s1, s2 = tf[0], tf[1], tf[2]

            def pool_front(g):
                dbuf = dbufs[g % 2]
                sP = load_unit([(0, wsrc(w_in, 0, 16, g * 256, 256), 16, 256)])
                wu = wv(sP, 0, 16, 256)
                kP_u = slots[sP].k((0, 4096))
                for ci in range(2):
                    segs = [(None, HALO, 0)] + [(tb, TB, HALO + tb * TB) for tb in range(2)]
                    for (tb, n, uoff) in segs:
                        b = nbank()
                        if tb is None:
                            rhs = [hhalo[:, k, :] for k in range(KC)]
                            rk = hhalo.k()
                        else:
                            rhs = [hB[:, k, tb * TB:(tb + 1) * TB] for k in range(KC)]
                            rk = hB.k(None, (tb * TB, (tb + 1) * TB))
                        mm_group(ps[b][:, 0:n], PK(b), [wu[:, k, ci * 128:(ci + 1) * 128] for k in range(KC)], rhs,
                                 kP_u + rk)
                        P.add("act", lambda e, b=b, n=n, uoff=uoff: e.activation(out=up[:, uoff:uoff + n],
                                                                                 in_=ps[b][:, 0:n], func=AF.Copy),
                              reads=PK(b), writes=up.k((uoff, uoff + n)))
                    L = HALO + T
                    srcv, lo = up, 0
                    bufs = [s1, s2]
                    for lvl in range(g + 1):
                        sh = 1 << lvl
                        dstv = bufs[lvl % 2]
                        nlo = lo + sh
                        P.add("dve", lambda e, srcv=srcv, dstv=dstv, nlo=nlo, sh=sh, L=L: e.tensor_tensor(
                            out=dstv[:, nlo:L], in0=srcv[:, nlo:L], in1=srcv[:, nlo - sh:L - sh], op=ALU.add),
                            reads=srcv.k((lo, L)), writes=dstv.k((nlo, L)))
                        srcv, lo = dstv, nlo
                    Sv = srcv
                    w = POOL_W[g]
                    P.add("dve", lambda e, Sv=Sv, ci=ci, w=w, L=L: e.scalar_tensor_tensor(
                        out=dbuf[:, ci, :], in0=Sv[:, HALO:L], scalar=1.0 / w, in1=up[:, HALO:L], op0=ALU.mult,
                        op1=ALU.subtract), reads=Sv.k((HALO, L)) + up.k((HALO, L)), writes=dbuf.k(ci))
                    ktmp = (("sm", "t16"),)
                    P.add("dve", lambda e, Sv=Sv, g=g, r=r: e.tensor_tensor(
                        out=sm[:, SM_T16:SM_T16 + 16], in0=Sv[:, HALO:HALO + 16],
                        in1=cv[:, CV_INVC + (r * 4 + g) * 16:CV_INVC + (r * 4 + g) * 16 + 16], op=ALU.mult),
                        reads=Sv.k((HALO, HALO + 16)) + kcv, writes=ktmp)
                    P.add("dve", lambda e, ci=ci: e.tensor_tensor(out=dbuf[:, ci, 0:16], in0=sm[:, SM_T16:SM_T16 + 16],
                                                                  in1=up[:, HALO:HALO + 16], op=ALU.subtract),
                          reads=ktmp + up.k((HALO, HALO + 16)) + dbuf.k(ci), writes=dbuf.k(ci))
                return sP

            def pool_back(g, sP):
                dbuf = dbufs[g % 2]
                kP_p = wpool_sb.k((2 * g, 2 * g + 2))
                for m in range(2):
                    for tb in range(2):
                        b = nbank()
                        mm_group(ps[b][:, :], PK(b), [wpool_sb[:, 2 * g + k, m * 128:(m + 1) * 128] for k in range(2)],
                                 [dbuf[:, k, tb * TB:(tb + 1) * TB] for k in range(2)],
                                 kP_p + dbuf.k(None, (tb * TB, (tb + 1) * TB)))
                        P.add("act", lambda e, b=b, tb=tb, cc=2 * g + m: e.activation(
                            out=ypool[:, cc, tb * TB:(tb + 1) * TB], in_=ps[b][:, :], func=AF.Identity,
                            scale=col(cv, CV_PSCALE + cc)),
                            reads=PK(b) + kcv, writes=ypool.k(2 * g + m, (tb * TB, (tb + 1) * TB)))


            khin, ksc, kccg = (("sm", "hin"),), (("sm", "sc"),), ccg.k()
            hin = sm[:, SM_HIN:SM_HIN + 16]
            scar = sm[:, SM_SC:SM_SC + 16]

            def fold():
                P.add("dve", lambda e: e.memset(hin, 0.0), writes=khin)
                for j in range(4):
                    P.add("dve", lambda e, j=j: e.scalar_tensor_tensor(out=hin, in0=scar, scalar=col(cv, CV_SEL + j),
                                                                       in1=hin, op0=ALU.mult, op1=ALU.add),
                          reads=khin + ksc + kcv, writes=khin)
                    gj = ccg[:, j, :].rearrange("p (c two) -> p c two", two=2)
                    P.add("dve", lambda e, gj=gj: e.tensor_tensor(out=scar, in0=scar, in1=gj[:, :, 1], op=ALU.mult),
                          reads=ksc + kccg, writes=ksc)
                    P.add("dve", lambda e, gj=gj: e.tensor_tensor(out=scar, in0=scar, in1=gj[:, :, 0], op=ALU.add),
                          reads=ksc + kccg, writes=ksc)

            sPs = {}
            for g in range(4):
                if g == 2:
                    exchange(sm[:, SM_CC1:SM_CC1 + 32], (("sm", "cc1"),), ccg)
                sPs[g] = pool_front(g)
                if g == 1:
                    P.add("pool", lambda e: e.dma_start(out=wpool_sb[:, :, :],
                                                        in_=w_pool.rearrange("(k p) n -> p k n", p=128)),
                          writes=wpool_sb.k(), kind="dma", dsem="wp")
                if g >= 1:
                    pool_back(g - 1, sPs[g - 1])
            mc0 = merge_ctx(0)
            merge_g(mc0, 0)
            merge_g(mc0, 1)
            pool_back(3, sPs[3])
            fold()

            for tb in range(2):
                for c in range(16):
                    P.add("dve", lambda e, c=c, tb=tb: e.scalar_tensor_tensor(
                        out=ylru[:, c, tb * TB:(tb + 1) * TB], in0=p2[:, c, tb * TB:(tb + 1) * TB],
                        scalar=col(sm, SM_HIN + c), in1=ylru[:, c, tb * TB:(tb + 1) * TB], op0=ALU.mult, op1=ALU.add),
                        reads=p2.k(c, (tb * TB, (tb + 1) * TB)) + ylru.k(c, (tb * TB, (tb + 1) * TB)) + khin,
                        writes=ylru.k(c, (tb * TB, (tb + 1) * TB)))

            for j in range(16):
                if j == 0:
                    mc = mc0
                else:
                    mc = merge_ctx(j)
                if j == 0:
                    merge_P(mc, 0)
                    merge_P(mc, 1)
                    merge_p2(mc, 0)
                    merge_p2(mc, 1)
                else:
                    for tb in range(2):
                        merge_g(mc, tb)
                        merge_P(mc, tb)
                        merge_p2(mc, tb)

            gv = p2
            def pair_ctx(grp, pr, alt=False):
                f0 = grp * 16 + 2 * pr
                if (grp, pr) in pre_ffn:
                    sU = pre_ffn[(grp, pr)]
                else:
                    sU = load_unit([(0, wsrc(w_up, 0, 16, f0 * 128, 256), 16, 256),
                                    (4096, wsrc(w_up, 0, 16, DFF + f0 * 128, 256), 16, 256)])
                wgt, wvl = wv(sU, 0, 16, 256), wv(sU, 4096, 16, 256)
                kU = slots[sU].k()
                res = []
                for fi in range(2):
                    f = f0 + fi
                    o = 2 * (f % 2)
                    base = 0 if alt else 4
                    res.append(dict(f=f, fl=f - grp * 16, gp=tf[base + o], cvb=tf[base + 1 + o], kU=kU,
                                    lw=[wgt[:, k, fi * 128:(fi + 1) * 128] for k in range(KC)],
                                    lv=[wvl[:, k, fi * 128:(fi + 1) * 128] for k in range(KC)]))
                return res

            def ffn_gate(c, tbs=(0, 1)):
                gp, lw, kU = c["gp"], c["lw"], c["kU"]
                for tb in tbs:
                    b = nbank()
                    mm_group(ps[b][:, :], PK(b), lw, [hB[:, k, tb * TB:(tb + 1) * TB] for k in range(KC)],
                             kU + hB.k(None, (tb * TB, (tb + 1) * TB)))
                    P.add("act", lambda e, b=b, gp=gp, tb=tb: e.activation(
                        out=gp[:, 2 + tb * TB:2 + (tb + 1) * TB], in_=ps[b][:, :], func=AF.Copy),
                        reads=PK(b), writes=gp.k((2 + tb * TB, 2 + (tb + 1) * TB)))

            def ffn_halo(c):
                gp, lw, kU = c["gp"], c["lw"], c["kU"]
                b = nbank()
                mm_group(ps[b][:, 0:2], PK(b), lw, [h2halo[:, k, :] for k in range(KC)], kU + h2halo.k())
                P.add("act", lambda e, b=b, gp=gp: e.activation(out=gp[:, 0:2], in_=ps[b][:, 0:2], func=AF.Copy),
                      reads=PK(b), writes=gp.k((0, 2)))

            def ffn_post(c):
                gp, cvb, f = c["gp"], c["cvb"], c["f"]
                kgp, kcb = gp.k((0, T + 2)), cvb.k((0, T))
                P.add("act", lambda e, gp=gp, cvb=cvb, f=f: e.activation(
                    out=cvb[:, 0:T], in_=gp[:, 2:2 + T], func=AF.Identity, scale=col(cv, CV_FCW + f * 3 + 2),
                    bias=col(cv, CV_FCB + f)), reads=kgp + kcv, writes=kcb)
                for kk in range(2):
                    P.add("dve", lambda e, gp=gp, cvb=cvb, f=f, kk=kk: e.scalar_tensor_tensor(
                        out=cvb[:, 0:T], in0=gp[:, kk:kk + T], scalar=col(cv, CV_FCW + f * 3 + kk),
                        in1=cvb[:, 0:T], op0=ALU.mult, op1=ALU.add), reads=kgp + kcb + kcv, writes=kcb)
                P.add("act", lambda e, cvb=cvb: e.activation(out=cvb[:, 0:T], in_=cvb[:, 0:T],
                                                             func=AF.Gelu_apprx_tanh), reads=kcb, writes=kcb)

            def ffn_valmm(c):
                banks = []
                for tb in range(2):
                    b = nbank()
                    mm_group(ps[b][:, :], PK(b), c["lv"], [hB[:, k, tb * TB:(tb + 1) * TB] for k in range(KC)],
                             c["kU"] + hB.k(None, (tb * TB, (tb + 1) * TB)))
                    banks.append(b)
                    held_banks.add(b)
                return banks

            def ffn_mult(c, banks, release=True):
                cvb, fl = c["cvb"], c["fl"]
                for tb in range(2):
                    b = banks[tb]
                    if release:
                        held_banks.discard(b)
                    P.add("dve", lambda e, b=b, cvb=cvb, fl=fl, tb=tb: e.tensor_tensor(
                        out=gv[:, fl, tb * TB:(tb + 1) * TB], in0=cvb[:, tb * TB:(tb + 1) * TB],
                        in1=ps[b][:, :], op=ALU.mult),
                        reads=PK(b) + cvb.k((tb * TB, (tb + 1) * TB)),
                        writes=gv.k(fl, (tb * TB, (tb + 1) * TB)))

            tt_order = [7, 0, 1, 2, 3, 4, 5, 6]
            kc2 = (("sm", "cc2"),)
            cc2 = sm[:, SM_CC2:SM_CC2 + 32]
            pre_ffn = {}
            for cb in range(4):
                sO = load_unit([(0, wsrc(w_out, 0, 16, cb * 512, 512), 16, 512)])
                wo = wv(sO, 0, 16, 512)
                kO = slots[sO].k()
                if cb == 3:
                    pre_ffn[(0, 0)] = load_unit([(0, wsrc(w_up, 0, 16, 0, 256), 16, 256),
                                                 (4096, wsrc(w_up, 0, 16, DFF, 256), 16, 256)])
                ctxs = {}
                for i, tt in enumerate(tt_order):
                    if cb == 3 and i >= 1:
                        tA = tt_order[i - 1]
                        ctxs[tA] = norm_A(xres[:, tA, :], xres.k(tA), 128)
                    xi = rot["xs"] % 4
                    rot["xs"] += 1
                    xsb = tf[xi]
                    kxs = xsb.k((0, TB))
                    P.add("sp", lambda e, xsb=xsb, tt=tt, cb=cb, r=r: e.dma_start(
                        out=xsb[:, 0:TB], in_=xh[r, HALO + tt * 128:HALO + (tt + 1) * 128, cb * 512:(cb + 1) * 512]),
                        writes=kxs, kind="dma", dsem="xs%d" % xi)
                    b = nbank()
                    tr = (tt * 128, (tt + 1) * 128)
                    mm_group(ps[b][:, :], PK(b), [merged[:, k, tt * 128:(tt + 1) * 128] for k in range(KC)],
                             [wo[:, k, :] for k in range(KC)], kO + merged.k(None, tr))
                    P.add("dve", lambda e, b=b, xsb=xsb, tt=tt, cb=cb: e.tensor_tensor(
                        out=xres[:, tt, cb * 512:(cb + 1) * 512], in0=ps[b][:, :], in1=xsb[:, 0:TB], op=ALU.add),
                        reads=PK(b) + kxs, writes=xres.k(tt, (cb * 512, (cb + 1) * 512)))
                    if cb == 3 and i >= 2:
                        t2 = tt_order[i - 2]
                        norm_B(ctxs[t2], CV_GMLP, hB, t2 * 128, 128)
                        if t2 == 7:
                            P.add("dve", lambda e: e.tensor_copy(out=cc2.rearrange("p (c two) -> p c two", two=2),
                                                                 in_=hB[:, :, T - 2:T]),
                                  reads=hB.k(None, (T - 2, T)), writes=kc2)
                            exchange(cc2, kc2, ccg2, in_queue="pool", defer_readback=True)
            t6, t5 = tt_order[-1], tt_order[-2]
            ctx6 = norm_A(xres[:, t6, :], xres.k(t6), 128)
            early = pair_ctx(0, 0, alt=True)
            ffn_gate(early[0], tbs=(0,))
            ffn_gate(early[1], tbs=(0,))
            norm_B(ctxs[t5], CV_GMLP, hB, t5 * 128, 128)
            norm_B(ctx6, CV_GMLP, hB, t6 * 128, 128)
            exchange_readback()

            khh, kgc, kccg2 = (("sm", "hh"),), (("sm", "gc"),), ccg2.k()
            hh = sm[:, SM_HH:SM_HH + 32]
            gcar = sm[:, SM_GC:SM_GC + 32]
            P.add("dve", lambda e: e.tensor_scalar(out=hh, in0=gcar, scalar1=col(cv, CV_SEL + 0), scalar2=None,
                                                   op0=ALU.mult), reads=kgc + kcv, writes=khh)
            for j in range(1, 4):
                P.add("dve", lambda e, j=j: e.scalar_tensor_tensor(out=hh, in0=ccg2[:, j - 1, :],
                                                                   scalar=col(cv, CV_SEL + j), in1=hh,
                                                                   op0=ALU.mult, op1=ALU.add),
                      reads=khh + kccg2 + kcv, writes=khh)
            P.add("dve", lambda e: e.tensor_copy(out=h2halo[:, :, :], in_=hh.rearrange("p (c two) -> p c two", two=2)),
                  reads=khh, writes=h2halo.k())
            P.add("dve", lambda e: e.tensor_copy(out=gcar, in_=ccg2[:, 3, :]), reads=kccg2, writes=kgc)

            for grp in range(3):
                pr_start = 0
                if grp == 0:
                    c0, c1 = early
                    c2, c3 = pair_ctx(0, 1)
                    ffn_gate(c0, tbs=(1,))
                    bk0 = ffn_valmm(c0)
                    ffn_gate(c1, tbs=(1,))
                    bk1 = ffn_valmm(c1)
                    ffn_gate(c2)
                    ffn_gate(c3)
                    ffn_halo(c0)
                    ffn_post(c0)
                    ffn_mult(c0, bk0, release=False)
                    ffn_halo(c1)
                    ffn_post(c1)
                    ffn_mult(c1, bk1, release=False)
                    for c in (c2, c3):
                        ffn_halo(c)
                        ffn_post(c)
                        ffn_mult(c, ffn_valmm(c))
                    held_banks.difference_update(bk0 + bk1)
                    pr_start = 2
                for pr in range(pr_start, 8):
                    for c in pair_ctx(grp, pr):
                        ffn_gate(c)
                        ffn_halo(c)
                        ffn_post(c)
                        ffn_mult(c, ffn_valmm(c))
                last = (grp == 2)
                nsteps = st0_steps(r + 1) if (last and r + 1 < R) else []
                for cb in range(4):
                    sD = load_unit([(0, wsrc(w_down, grp * 2048, 16, cb * 512, 512), 16, 512)])
                    wd = wv(sD, 0, 16, 512)
                    kD = slots[sD].k()
                    fuse = last and cb == 3
                    order = [6, 7, 0, 1, 2, 3, 4, 5] if fuse else list(range(NT))
                    prev = None
                    for tt in order:
                        b = nbank()
                        tr = (tt * 128, (tt + 1) * 128)
                        mm_group(ps[b][:, :], PK(b), [gv[:, k, tt * 128:(tt + 1) * 128] for k in range(KC)],
                                 [wd[:, k, :] for k in range(KC)], kD + gv.k(None, tr))
                        kx = xres.k(tt, (cb * 512, (cb + 1) * 512))
                        P.add("dve", lambda e, b=b, tt=tt, cb=cb: e.tensor_tensor(
                            out=xres[:, tt, cb * 512:(cb + 1) * 512], in0=ps[b][:, :],
                            in1=xres[:, tt, cb * 512:(cb + 1) * 512], op=ALU.add), reads=PK(b) + kx, writes=kx)
                        if fuse:
                            ctx = final_A(tt)
                            if prev is not None:
                                final_B(prev)
                            prev = ctx
                        elif nsteps and tt % 2 == 0:
                            nsteps.pop(0)()
                    if last and cb == 2:
                        while nsteps:
                            nsteps.pop(0)()
                        if r + 1 < R:
                            st0_done.add(r + 1)
                    if fuse:
                        final_B(prev)

        fin_o0, fin_o1 = P.dsem_val["o0"], P.dsem_val["o1"]

        def fin(e):
            e.wait_ge(dsems["o0"], fin_o0)
            e.wait_ge(dsems["o1"], fin_o1)
            return e.nop()
        P.add("sp", fin)

        P.finalize()

        @block.tensor
        def _(e):
            P.emit("pe", e, engsem, dsems)

        @block.scalar
        def _(e):
            P.emit("act", e, engsem, dsems)

        @block.vector
        def _(e):
            P.emit("dve", e, engsem, dsems)

        @block.gpsimd
        def _(e):
            P.emit("pool", e, engsem, dsems)

        @block.sync
        def _(e):
            P.emit("sp", e, engsem, dsems)
    return nc


def _fm(v, nch):
    return np.ascontiguousarray(np.asarray(v, np.float32).reshape(nch, 128).T)


_NC_CACHE = {}


def kernel(x, g_mix, w_in, b_gate, w_pool, pool_scale, lru_conv_w, lru_conv_b, w_a, b_a, w_i, b_i, lru_lambda,
           w_pool_proj, w_lru_proj, w_out, g_mlp, w_up, ffn_conv_w, ffn_conv_b, w_down, g_final):
    x = np.asarray(x, np.float32)
    B, S, _ = x.shape
    f = lambda a: np.ascontiguousarray(np.asarray(a, np.float32))
    cvb = np.zeros((128, NCV), np.float32)
    cvb[:, CV_GMIX:CV_GMIX + 16] = _fm(g_mix[0], 16)
    cvb[:, CV_GMLP:CV_GMLP + 16] = _fm(g_mlp[0], 16)
    cvb[:, CV_BGATE:CV_BGATE + 32] = _fm(b_gate[0], 32)
    cvb[:, CV_PSCALE:CV_PSCALE + 8] = _fm(pool_scale[0], 8)
    lcw = np.asarray(lru_conv_w[0], np.float32)
    cvb[:, CV_LCW:CV_LCW + 64] = lcw.reshape(4, 16, 128).transpose(2, 1, 0).reshape(128, 64)
    cvb[:, CV_LCB:CV_LCB + 16] = _fm(lru_conv_b[0], 16)
    cvb[:, CV_BA:CV_BA + 16] = _fm(b_a[0], 16)
    cvb[:, CV_BI:CV_BI + 16] = _fm(b_i[0], 16)
    cvb[:, CV_LAM:CV_LAM + 16] = _fm(lru_lambda[0], 16)
    fcw = np.asarray(ffn_conv_w[0], np.float32)
    cvb[:, CV_FCW:CV_FCW + 144] = fcw.reshape(3, 48, 128).transpose(2, 1, 0).reshape(128, 144)
    cvb[:, CV_FCB:CV_FCB + 48] = _fm(ffn_conv_b[0], 48)
    gfb = np.ascontiguousarray(np.broadcast_to(np.asarray(g_final, np.float32)[None, :], (128, D)))
    shared = {
        "gfin": gfb, "w_in": f(w_in[0]), "w_pool": f(w_pool[0]).reshape(1024, 256),
        "w_a": f(w_a[0]).reshape(2048, 256), "w_i": f(w_i[0]).reshape(2048, 256),
        "w_pool_proj": f(w_pool_proj[0]), "w_lru_proj": f(w_lru_proj[0]), "w_out": f(w_out[0]),
        "w_up": f(w_up[0]), "w_down": f(w_down[0]),
    }
    in_maps = []
    for c in range(NCORES):
        b, q = c // 4, c % 4
        xhc = np.zeros((R, T + HALO, D), np.float32)
        cvc = cvb.copy()
        cvc[:, CV_SEL + q] = 1.0
        for r in range(R):
            p = 4 * r + q
            t0 = p * T
            xhc[r, HALO:] = x[b, t0:t0 + T]
            if t0 > 0:
                xhc[r, :HALO] = x[b, t0 - HALO:t0]
            for g, w in enumerate(POOL_W):
                cnt = np.minimum(np.arange(t0 + 1, t0 + 17), w).astype(np.float32)
                cvc[:, CV_INVC + (r * 4 + g) * 16:CV_INVC + (r * 4 + g) * 16 + 16] = (1.0 / cnt)[None, :]
        m = dict(shared)
        m["xh"] = xhc
        m["cv"] = cvc
        in_maps.append(m)
    if "nc" not in _NC_CACHE:
        _NC_CACHE["nc"] = build_nc()
    nc = _NC_CACHE["nc"]
    res = run_bass_kernel_spmd(nc, in_maps, core_ids=list(range(NCORES)))
    out = np.empty((B, S, D), np.float32)
    for c in range(NCORES):
        b, q = c // 4, c % 4
        oc = np.asarray(res.results[c]["out"]).reshape(R, T, D)
        for r in range(R):
            p = 4 * r + q
            out[b, p * T:(p + 1) * T] = oc[r]
    return out
```
